# Optimizing a Trainium2 kernel written in Bass

```python
import math
import jax
import jax.numpy as jnp
from jax import lax
import numpy as np

D_MODEL = 1024
BATCH = 16
SEQ = 2048
DEPTH = 4

N_MIXERS = 4
NORM_EPS = 1e-6
ROPE_THETA = 10000.0
HEAD_DIM = 64
NEG_INF = -1e30

POOL_WINDOWS = (2, 4, 8, 16)
POOL_GROUP = D_MODEL // len(POOL_WINDOWS)

DIFF_HEADS = D_MODEL // (2 * HEAD_DIM)
DIFF_SUBLN_EPS = 1e-5
Q_BLOCK = 128

LRU_WIDTH = D_MODEL
LRU_BLOCKS = D_MODEL // HEAD_DIM
LRU_BLOCK_DIM = LRU_WIDTH // LRU_BLOCKS
LRU_CONV = 4
LRU_C = 8.0

DIL_GROUPS = ((128, 1), (512, 4), (2048, 16))
DIL_HEADS = D_MODEL // HEAD_DIM
DIL_WIDTH = DIL_HEADS * HEAD_DIM

D_FF = 2816
FFN_CONV = 3

kernel_name = "hybrid_pool_diffattn_rglru_dilated_encoder"


def _n_layers_of(mixer):
    return len(range(mixer, DEPTH, N_MIXERS))


def _rms_norm(x, gain, eps=NORM_EPS):
    xf = x.astype(jnp.float32)
    y = xf * lax.rsqrt(jnp.mean(xf * xf, axis=-1, keepdims=True) + eps)
    return (y * gain.astype(jnp.float32)).astype(x.dtype)


def _rope_tables(positions, head_dim):
    inv = ROPE_THETA ** (-jnp.arange(0, head_dim, 2, dtype=jnp.float32) / head_dim)
    ang = positions.astype(jnp.float32)[:, None] * inv[None, :]
    return jnp.cos(ang), jnp.sin(ang)


def _rope(x, cos, sin):
    shape = (cos.shape[0],) + (1,) * (x.ndim - 3) + (cos.shape[1],)
    c, s = cos.reshape(shape), sin.reshape(shape)
    xf = x.astype(jnp.float32)
    x1, x2 = jnp.split(xf, 2, axis=-1)
    return jnp.concatenate([x1 * c - x2 * s, x2 * c + x1 * s], axis=-1).astype(x.dtype)


def _pool_mixer(x, w, scale):
    b, s, d = x.shape
    xf = x.astype(jnp.float32)
    cs = jnp.concatenate([jnp.zeros((b, 1, d), jnp.float32), lax.cumsum(xf, axis=1)], axis=1)
    t = jnp.arange(s)
    parts = []
    for g, win in enumerate(POOL_WINDOWS):
        lo = jnp.clip(t - win // 2, 0, s)
        hi = jnp.clip(t + win - win // 2, 0, s)
        sl = slice(g * POOL_GROUP, (g + 1) * POOL_GROUP)
        csg = cs[:, :, sl]
        mean = (csg[:, hi] - csg[:, lo]) / (hi - lo).astype(jnp.float32)[:, None]
        parts.append(mean - xf[:, :, sl])
    pooled = jnp.stack(parts, axis=2).astype(x.dtype)
    y = jnp.einsum('bsgc,gce->bsge', pooled, w).reshape(b, s, d)
    return y * scale


def _diff_attention(x, w_qkv, lq1, lk1, lq2, lk2, subln, w_o, cos, sin, layer_idx):
    b, s, _ = x.shape
    h, dh = DIFF_HEADS, HEAD_DIM
    q, k, v = jnp.split(x @ w_qkv, 3, axis=-1)
    q = _rope(q.reshape(b, s, 2 * h, dh), cos, sin).reshape(b, s, h, 2, dh) * (dh ** -0.5)
    k = _rope(k.reshape(b, s, 2 * h, dh), cos, sin).reshape(b, s, h, 2, dh)
    v = v.reshape(b, s, h, 2 * dh)
    lam_init = 0.8 - 0.6 * math.exp(-0.3 * layer_idx)
    lam = (jnp.exp(jnp.sum(lq1.astype(jnp.float32) * lk1.astype(jnp.float32)))
           - jnp.exp(jnp.sum(lq2.astype(jnp.float32) * lk2.astype(jnp.float32))) + lam_init)
    nb = s // Q_BLOCK
    q_blocks = q.reshape(b, nb, Q_BLOCK, h, 2, dh).swapaxes(0, 1)

    def one_block(qb):
        sc = jnp.einsum('bqhce,bkhce->bhcqk', qb, k).astype(jnp.float32)
        p = jax.nn.softmax(sc, axis=-1)
        a = p[:, :, 0] - lam * p[:, :, 1]
        return jnp.einsum('bhqk,bkhe->bqhe', a.astype(v.dtype), v)

    o = lax.map(one_block, q_blocks).swapaxes(0, 1).reshape(b, s, h, 2 * dh)
    o = _rms_norm(o, subln, DIFF_SUBLN_EPS) * (1.0 - lam_init)
    return o.reshape(b, s, h * 2 * dh) @ w_o


def _causal_conv(x, w, bias):
    kw = w.shape[0]
    s = x.shape[1]
    xp = jnp.pad(x, ((0, 0), (kw - 1, 0), (0, 0)))
    y = bias
    for j in range(kw):
        y = y + w[j] * xp[:, j:j + s]
    return y


def _rg_lru_direction(u, conv_w, conv_b, w_a, b_a, w_x, b_x, lam):
    b, s, c = u.shape
    xc = _causal_conv(u, conv_w, conv_b)
    xb = xc.reshape(b, s, LRU_BLOCKS, LRU_BLOCK_DIM)
    r = jax.nn.sigmoid((jnp.einsum('bsni,nij->bsnj', xb, w_a).reshape(b, s, c) + b_a).astype(jnp.float32))
    i = jax.nn.sigmoid((jnp.einsum('bsni,nij->bsnj', xb, w_x).reshape(b, s, c) + b_x).astype(jnp.float32))
    log_a = LRU_C * r * jax.nn.log_sigmoid(lam.astype(jnp.float32))
    a = jnp.exp(log_a)
    inp = jnp.sqrt(-jnp.expm1(2.0 * log_a)) * (i * xc.astype(jnp.float32))

    def combine(left, right):
        a_l, b_l = left
        a_r, b_r = right
        return a_l * a_r, a_r * b_l + b_r

    _, h = lax.associative_scan(combine, (a, inp), axis=1)
    return h


def _rglru_mixer(x, w_in, conv_w, conv_b, w_a, b_a, w_x, b_x, lam, w_out):
    gate, u = jnp.split(x @ w_in, 2, axis=-1)
    h_fwd = _rg_lru_direction(u, conv_w[0], conv_b[0], w_a[0], b_a[0], w_x[0], b_x[0], lam[0])
    h_bwd = jnp.flip(_rg_lru_direction(jnp.flip(u, axis=1), conv_w[1], conv_b[1], w_a[1], b_a[1],
                                       w_x[1], b_x[1], lam[1]), axis=1)
    y = jax.nn.gelu(gate.astype(jnp.float32), approximate=True) * (h_fwd + h_bwd)
    return y.astype(x.dtype) @ w_out


def _dilated_chain_attention(q, k, v, dilation, half):
    b, s, h, dh = q.shape
    n = s // dilation
    blk = half
    nb = -(-n // blk)
    pad = nb * blk - n

    def chains(t):
        return t.reshape(b, n, dilation, h, dh).transpose(0, 2, 3, 1, 4)

    qc = jnp.pad(chains(q), ((0, 0), (0, 0), (0, 0), (0, pad), (0, 0))).reshape(b, dilation, h, nb, blk, dh)

    def windows(t):
        tp = jnp.pad(chains(t), ((0, 0), (0, 0), (0, 0), (blk, blk + pad), (0, 0)))
        tp = tp.reshape(b, dilation, h, nb + 2, blk, dh)
        return jnp.concatenate([tp[:, :, :, :-2], tp[:, :, :, 1:-1], tp[:, :, :, 2:]], axis=4)

    kw, vw = windows(k), windows(v)
    qpos = jnp.arange(nb * blk).reshape(nb, blk)
    kpos = jnp.arange(nb)[:, None] * blk + jnp.arange(-blk, 2 * blk)[None, :]
    rel = kpos[:, None, :] - qpos[:, :, None]
    valid = (jnp.abs(rel) <= half) & (kpos[:, None, :] >= 0) & (kpos[:, None, :] < n)
    sc = jnp.einsum('bdhnqe,bdhnke->bdhnqk', qc, kw).astype(jnp.float32)
    sc = jnp.where(valid, sc, NEG_INF)
    lse = jax.nn.logsumexp(sc, axis=-1)
    p = jnp.exp(sc - lse[..., None])
    o = jnp.einsum('bdhnqk,bdhnke->bdhnqe', p.astype(v.dtype), vw)

    def unchain(t):
        t = t.reshape((b, dilation, h, nb * blk) + t.shape[5:])[:, :, :, :n]
        t = jnp.moveaxis(t, 3, 1)
        return t.reshape((b, s, h) + t.shape[4:])

    return unchain(o), unchain(lse)


def _dilated_mixer(x, w_qkv, w_o, cos, sin):
    b, s, _ = x.shape
    n_groups = len(DIL_GROUPS)
    qkv = (x @ w_qkv).reshape(b, s, n_groups, 3, DIL_HEADS, HEAD_DIM)
    q = _rope(qkv[:, :, :, 0], cos, sin) * (HEAD_DIM ** -0.5)
    k = _rope(qkv[:, :, :, 1], cos, sin)
    v = qkv[:, :, :, 2]
    outs, lses = [], []
    for g, (window, dilation) in enumerate(DIL_GROUPS):
        o, l = _dilated_chain_attention(q[:, :, g], k[:, :, g], v[:, :, g], dilation, window // (2 * dilation))
        outs.append(o.astype(jnp.float32))
        lses.append(l)
    wts = jax.nn.softmax(jnp.stack(lses, axis=0), axis=0)
    o = jnp.sum(wts[..., None] * jnp.stack(outs, axis=0), axis=0)
    return o.astype(x.dtype).reshape(b, s, DIL_WIDTH) @ w_o


def _conv_ffn(x, w_up, conv_w, conv_b, w_down):
    s = x.shape[1]
    g, u = jnp.split(x @ w_up, 2, axis=-1)
    half = FFN_CONV // 2
    gp = jnp.pad(g, ((0, 0), (half, half), (0, 0)))
    gc = conv_b
    for j in range(FFN_CONV):
        gc = gc + conv_w[j] * gp[:, j:j + s]
    act = jax.nn.gelu(gc.astype(jnp.float32), approximate=False).astype(x.dtype)
    return (act * u) @ w_down


def setup_inputs(seed: int = 0) -> dict:
    key = jax.random.key(seed)
    keys = iter(jax.random.split(key, 48))
    f32 = jnp.float32
    n_a, n_b, n_c, n_d = (_n_layers_of(m) for m in range(N_MIXERS))

    def dense(shape, fan_in):
        return jax.random.normal(next(keys), shape, f32) * fan_in ** -0.5

    def gain(shape):
        return 1.0 + 0.05 * jax.random.normal(next(keys), shape, f32)

    def small(shape, scale=0.02):
        return scale * jax.random.normal(next(keys), shape, f32)

    x = jax.random.normal(next(keys), (BATCH, SEQ, D_MODEL), f32)
    a_c = jax.random.uniform(next(keys), (n_c, 2, LRU_WIDTH), f32, 0.9, 0.999)
    base = a_c ** (1.0 / LRU_C)
    lru_lambda = jnp.log(base) - jnp.log1p(-base)
    return {
        "x": x,
        "positions": jnp.arange(SEQ, dtype=jnp.int32),
        "mix_norm": gain((DEPTH, D_MODEL)),
        "pool_w": dense((n_a, len(POOL_WINDOWS), POOL_GROUP, POOL_GROUP), POOL_GROUP),
        "pool_scale": gain((n_a, D_MODEL)),
        "diff_w_qkv": dense((n_b, D_MODEL, 3 * D_MODEL), D_MODEL),
        "diff_lam_q1": small((n_b, HEAD_DIM), 0.1),
        "diff_lam_k1": small((n_b, HEAD_DIM), 0.1),
        "diff_lam_q2": small((n_b, HEAD_DIM), 0.1),
        "diff_lam_k2": small((n_b, HEAD_DIM), 0.1),
        "diff_subln": gain((n_b, 2 * HEAD_DIM)),
        "diff_w_o": dense((n_b, D_MODEL, D_MODEL), D_MODEL),
        "lru_w_in": dense((n_c, D_MODEL, 2 * LRU_WIDTH), D_MODEL),
        "lru_conv_w": dense((n_c, 2, LRU_CONV, LRU_WIDTH), LRU_CONV),
        "lru_conv_b": small((n_c, 2, LRU_WIDTH)),
        "lru_w_a": dense((n_c, 2, LRU_BLOCKS, LRU_BLOCK_DIM, LRU_BLOCK_DIM), LRU_BLOCK_DIM),
        "lru_b_a": small((n_c, 2, LRU_WIDTH)),
        "lru_w_x": dense((n_c, 2, LRU_BLOCKS, LRU_BLOCK_DIM, LRU_BLOCK_DIM), LRU_BLOCK_DIM),
        "lru_b_x": small((n_c, 2, LRU_WIDTH)),
        "lru_lambda": lru_lambda,
        "lru_w_out": dense((n_c, LRU_WIDTH, D_MODEL), LRU_WIDTH),
        "dil_w_qkv": dense((n_d, D_MODEL, len(DIL_GROUPS) * 3 * DIL_WIDTH), D_MODEL),
        "dil_w_o": dense((n_d, DIL_WIDTH, D_MODEL), DIL_WIDTH),
        "ffn_norm": gain((DEPTH, D_MODEL)),
        "ffn_w_up": dense((DEPTH, D_MODEL, 2 * D_FF), D_MODEL),
        "ffn_conv_w": dense((DEPTH, FFN_CONV, D_FF), FFN_CONV),
        "ffn_conv_b": small((DEPTH, D_FF)),
        "ffn_w_down": dense((DEPTH, D_FF, D_MODEL), D_FF),
        "final_norm": gain((D_MODEL,)),
    }


def reference(x, positions, mix_norm, pool_w, pool_scale, diff_w_qkv, diff_lam_q1, diff_lam_k1,
              diff_lam_q2, diff_lam_k2, diff_subln, diff_w_o, lru_w_in, lru_conv_w, lru_conv_b,
              lru_w_a, lru_b_a, lru_w_x, lru_b_x, lru_lambda, lru_w_out, dil_w_qkv, dil_w_o,
              ffn_norm, ffn_w_up, ffn_conv_w, ffn_conv_b, ffn_w_down, final_norm):
    cos, sin = _rope_tables(positions, HEAD_DIM)
    h = x
    for i in range(DEPTH):
        m, j = i % N_MIXERS, i // N_MIXERS
        hn = _rms_norm(h, mix_norm[i])
        if m == 0:
            y = _pool_mixer(hn, pool_w[j], pool_scale[j])
        elif m == 1:
            y = _diff_attention(hn, diff_w_qkv[j], diff_lam_q1[j], diff_lam_k1[j], diff_lam_q2[j],
                                diff_lam_k2[j], diff_subln[j], diff_w_o[j], cos, sin, i)
        elif m == 2:
            y = _rglru_mixer(hn, lru_w_in[j], lru_conv_w[j], lru_conv_b[j], lru_w_a[j], lru_b_a[j],
                             lru_w_x[j], lru_b_x[j], lru_lambda[j], lru_w_out[j])
        else:
            y = _dilated_mixer(hn, dil_w_qkv[j], dil_w_o[j], cos, sin)
        h = h + y
        h = h + _conv_ffn(_rms_norm(h, ffn_norm[i]), ffn_w_up[i], ffn_conv_w[i], ffn_conv_b[i], ffn_w_down[i])
    return _rms_norm(h, final_norm)
```

```python
import math
from contextlib import ExitStack

import numpy as np
import concourse.bass as bass
import concourse.mybir as mybir
from concourse.bass_utils import run_bass_kernel_spmd

F32 = mybir.dt.float32
BF16 = mybir.dt.bfloat16
I32 = mybir.dt.int32
AF = mybir.ActivationFunctionType
ALU = mybir.AluOpType
AX = mybir.AxisListType

NCORES = 8
S = 2048
D = 1024
KC = 8
DFF = 2816
FC = 22
NSEQ = 2
EPS = 1e-6
SLOT = 2816
NSLOT = 5


class Res:
    __slots__ = ("name", "writer", "readers")

    def __init__(self, name):
        self.name = name
        self.writer = None
        self.readers = {}


class EngW:
    def __init__(self, name, eng, sem):
        self.name = name
        self.eng = eng
        self.sem = sem
        self.cnt = 0
        self.known = {}
        self.ring = []
        self.ri = 0


class DSem:
    def __init__(self, sem):
        self.sem = sem
        self.cnt = 0


class Prog:
    def __init__(self, nc, es):
        self.nc = nc
        self.es = es
        mk = lambda n, e: EngW(n, e, es.enter_context(nc.semaphore("sem_" + n)))
        self.pe = mk("pe", nc.tensor)
        self.act = mk("act", nc.scalar)
        self.dve = mk("dve", nc.vector)
        self.pool = mk("pool", nc.gpsimd)
        self.sp = mk("sp", nc.sync)
        self.engs = [self.pe, self.act, self.dve, self.pool, self.sp]
        self.dsems = []
        for q, n in ((self.pool, 8), (self.sp, 6)):
            for i in range(n):
                ds = DSem(es.enter_context(nc.semaphore(f"dq_{q.name}{i}")))
                q.ring.append(ds)
                self.dsems.append(ds)

    def _wait(self, E, owner, val):
        if val <= 0:
            return
        k = id(owner)
        if E.known.get(k, 0) >= val:
            return
        if owner is E:
            assert val <= E.cnt, f"self-wait deadlock on {E.name}"
        E.eng.wait_ge(owner.sem, val)
        E.known[k] = val

    def _deps(self, E, reads, writes, skip_same):
        for r in reads:
            if r.writer is not None:
                self._wait(E, *r.writer)
        for w in writes:
            if w.writer is not None and not (skip_same and w.writer[0] is E and E is self.pe):
                self._wait(E, *w.writer)
            for (o, v) in w.readers.values():
                if not (skip_same and o is E):
                    self._wait(E, o, v)

    def _record(self, tok, reads, writes):
        for r in reads:
            k = id(tok[0])
            if k not in r.readers or r.readers[k][1] < tok[1]:
                r.readers[k] = tok
        for w in writes:
            w.writer = tok
            w.readers = {}

    def op(self, E, fn, reads=(), writes=(), signal=True):
        self._deps(E, reads, writes, True)
        ins = fn(E.eng)
        if signal:
            E.cnt += 1
            ins.then_inc(E.sem, 1)
            tok = (E, E.cnt)
        else:
            tok = (E, E.cnt + 1)
        self._record(tok, reads, writes)
        return tok

    def dma(self, Q, out, in_, reads=(), writes=()):
        ds = Q.ring[Q.ri % len(Q.ring)]
        Q.ri += 1
        self._wait(Q, ds, ds.cnt * 16)
        self._deps(Q, reads, writes, False)
        ins = Q.eng.dma_start(out=out, in_=in_)
        ds.cnt += 1
        ins.then_inc(ds.sem, 16)
        tok = (ds, ds.cnt * 16)
        self._record(tok, reads, writes)
        return tok

    def barrier(self):
        for E in self.engs:
            for O in self.engs:
                if O is not E:
                    self._wait(E, O, O.cnt)
            for ds in self.dsems:
                self._wait(E, ds, ds.cnt * 16)

    def mm_group(self, out, pairs, reads, writes):
        n = len(pairs)
        tok = None
        for i, (l, r) in enumerate(pairs):
            tok = self.op(self.pe,
                          lambda e, l=l, r=r, i=i: e.matmul(out, l, r, start=(i == 0), stop=(i == n - 1)),
                          reads=reads, writes=writes, signal=(i == n - 1))
        return tok


class WStream:
    def __init__(self, P, ring_t, nslot):
        self.P = P
        self.ring_t = ring_t
        self.nslot = nslot
        self.res = [Res(f"wslot{i}") for i in range(nslot)]
        self.items = []
        self.issued = 0
        self.consumed = 0

    def plan(self, tag, src, nelem):
        assert nelem <= SLOT
        self.items.append((tag, src, nelem))

    def _issue_upto(self, n):
        n = min(n, len(self.items))
        while self.issued < n:
            i = self.issued
            tag, src, nelem = self.items[i]
            s = i % self.nslot
            dst = self.ring_t[:, s, 0:nelem]
            shp = src.shape
            if len(shp) == 3:
                dst = dst.rearrange("p (a b) -> p a b", a=shp[1])
            elif len(shp) == 4:
                dst = dst.rearrange("p (a b c) -> p a b c", a=shp[1], b=shp[2])
            self.P.dma(self.P.pool, dst, src, writes=[self.res[s]])
            self.issued += 1

    def get(self, tag):
        i = self.consumed
        t, src, nelem = self.items[i]
        assert t == tag, (t, tag)
        self._issue_upto(i + 1)
        s = i % self.nslot
        self.consumed += 1
        view = self.ring_t[:, s, 0:nelem]
        return view, self.res[s]

    def release(self):
        self._issue_upto(self.consumed + self.nslot - 1)


class ParamPack:
    def __init__(self):
        self.cols = {}
        self.n = 0
        self.parts = []

    def add(self, name, arr2d):
        arr2d = np.ascontiguousarray(arr2d, dtype=np.float32)
        assert arr2d.shape[0] == 128
        self.cols[name] = (self.n, arr2d.shape[1])
        self.n += arr2d.shape[1]
        self.parts.append(arr2d)

    def build(self):
        return np.ascontiguousarray(np.concatenate(self.parts, axis=1))


def chan_layout(v, nk):
    v = np.asarray(v, dtype=np.float32)
    lead = v.shape[:-1]
    v = v.reshape(-1, nk, 128)
    return np.transpose(v, (2, 0, 1)).reshape(128, -1)


def pack_params(inp):
    pp = ParamPack()
    pp.add("mix_norm", chan_layout(inp["mix_norm"], KC))
    pp.add("ffn_norm", chan_layout(inp["ffn_norm"], KC))
    pp.add("final_norm", chan_layout(inp["final_norm"], KC))
    pp.add("pool_scale", chan_layout(inp["pool_scale"][0], KC))
    pp.add("ffn_conv_w", chan_layout(inp["ffn_conv_w"], FC))
    pp.add("ffn_conv_b", chan_layout(inp["ffn_conv_b"], FC))
    p = np.arange(128)
    e_ = p % 64
    quad, j = e_ // 32, e_ % 32
    fi = (j % 16) + 16 * quad
    inv = (np.float32(10000.0) ** (-(np.arange(0, 64, 2, dtype=np.float32)) / np.float32(64))).astype(np.float32)
    pp.add("inv_freq", inv[fi][:, None])
    pp.add("rope_sgn", np.where(j < 16, -1.0, 1.0).astype(np.float32)[:, None])
    bc = lambda v: np.broadcast_to(np.asarray(v, np.float32).reshape(1, -1), (128, np.asarray(v).size))
    pp.add("lam_q1", bc(inp["diff_lam_q1"][0]))
    pp.add("lam_k1", bc(inp["diff_lam_k1"][0]))
    pp.add("lam_q2", bc(inp["diff_lam_q2"][0]))
    pp.add("lam_k2", bc(inp["diff_lam_k2"][0]))
    pp.add("subln", bc(inp["diff_subln"][0]))
    pp.add("lru_conv_w", chan_layout(inp["lru_conv_w"][0], KC))
    pp.add("lru_conv_b", chan_layout(inp["lru_conv_b"][0], KC))
    pp.add("lru_b_a", chan_layout(inp["lru_b_a"][0], KC))
    pp.add("lru_b_x", chan_layout(inp["lru_b_x"][0], KC))
    pp.add("lru_lambda", chan_layout(inp["lru_lambda"][0], KC))
    return pp


def rope_perm64():
    e_ = np.arange(64)
    quad, j = e_ // 32, e_ % 32
    fi = (j % 16) + 16 * quad
    return np.where(j < 16, fi, 32 + fi)


def permute_qk_cols(w, qk_blocks):
    w = np.array(w, dtype=np.float32, copy=True)
    perm = rope_perm64()
    for b0 in qk_blocks:
        w[:, b0:b0 + 64] = w[:, b0 + perm]
    return w


def lru_blockdiag(w_a, w_x):
    bd = np.zeros((128, 8, 2, 2, 128), np.float32)
    for d in range(2):
        for ax, w in enumerate((w_a, w_x)):
            for c in range(8):
                for hb in range(2):
                    bd[hb * 64:(hb + 1) * 64, c, d, ax, hb * 64:(hb + 1) * 64] = w[d, 2 * c + hb]
    return bd


class Builder:
    def __init__(self, pp_cols, npar, stop_after=None, mixers=(0, 1, 2, 3), ffns=(0, 1, 2, 3), nseq=NSEQ):
        self.pp_cols = pp_cols
        self.npar = npar
        self.stop_after = stop_after
        self.mixers = set(mixers)
        self.ffns = set(ffns)
        self.nseq = nseq
        self.nc = bass.Bass("TRN2", target_bir_lowering=False)

    def uq(self):
        self._uq = getattr(self, "_uq", 0) + 1
        return f"t{self._uq}_"

    def par(self, name, idx, width=1):
        c0, w = self.pp_cols[name]
        assert idx + width <= w
        return self.params[:, c0 + idx:c0 + idx + width]

    def plan_weights(self):
        ws = self.ws
        for sq in range(self.nseq):
            for li in range(4):
                if self.stop_after is not None and li > self.stop_after:
                    break
                if li == 0 and li in self.mixers:
                    for g in range(4):
                        ws.plan(("pool", g), self.dram["pool_w"][0, g].rearrange("(kc p) e -> p kc e", p=128), 512)
                if li == 1 and li in self.mixers:
                    wq = self.dram["diff_w_qkv_p"].rearrange("(k p) (t hh c) -> p k t hh c", p=128, t=3, hh=8, c=128)
                    wv_ = self.dram["diff_w_qkv_p"].rearrange("(k p) (t gg c) -> p k t gg c", p=128, t=3, gg=4, c=256)
                    wo_ = self.dram["diff_w_o"][0].rearrange("(gg kc p) n -> p gg kc n", p=128, kc=2)
                    for gi in range(4):
                        for which, ti in (("q", 0), ("k", 1)):
                            for hl in range(2):
                                ws.plan(("d" + which, gi, hl), wq[:, :, ti, gi * 2 + hl, :], 1024)
                        ws.plan(("dv", gi), wv_[:, :, 2, gi, :], 2048)
                        ws.plan(("do", gi), wo_[:, gi, :, :], 2048)
                if li == 2 and li in self.mixers:
                    win = self.dram["lru_w_in"][0].rearrange("(k p) (two cc c) -> p two k cc c", p=128, two=2, cc=8, c=128)
                    wout = self.dram["lru_w_out"][0].rearrange("(k p) (mp c) -> p k mp c", p=128, c=256)
                    for c in range(KC):
                        ws.plan(("lin", c), win[:, :, :, c, :], 2048)
                        ws.plan(("lbd", c), self.dram["lru_bd"][:, c], 512)
                    for mp in range(4):
                        ws.plan(("lout", mp), wout[:, :, mp, :], 2048)
                if li == 3 and li in self.mixers:
                    wx = self.dram["dil_w_qkv_p"].rearrange("(k p) (g t hh c) -> p g t k hh c", p=128, g=3, t=3, hh=8, c=128)
                    for c in range(8):
                        for g in range(3):
                            ws.plan(("xq", c, g), wx[:, g, 0, :, c, :], 1024)
                            ws.plan(("xk", c, g), wx[:, g, 1, :, c, :], 1024)
                            ws.plan(("xv", c, g), wx[:, g, 2, :, c, :], 1024)
                        ws.plan(("xo", c), self.dram["dil_w_o"][0][c * 128:(c + 1) * 128, :], 1024)
                if li not in self.ffns:
                    continue
                wup = self.dram["ffn_w_up"][li].rearrange("(k p) (two fc c) -> p two k fc c", p=128, two=2, fc=FC, c=128)
                wdn = self.dram["ffn_w_down"][li].rearrange("(j p) (m c) -> p j m c", p=128, c=128)
                for hf in range(2):
                    for f in range(FC):
                        ws.plan(("up", li, hf, f), wup[:, :, :, f, :], 2048)
                    for m in range(KC):
                        ws.plan(("dn", li, hf, m), wdn[:, :, m, :], 2816)

    def rms_stats(self, rstd, rstdR, ps_bank, psR, sq, sqR):
        P = self.P
        for t in range(4):
            ts = slice(t * 512, (t + 1) * 512)
            for k in range(KC):
                b = k % 2
                P.op(P.act, lambda e, k=k, b=b: e.activation(out=sq[b], in_=self.h[:, k, ts], func=AF.Square),
                     reads=[self.hR[k][t]], writes=[sqR[b]])
                P.op(P.pe, lambda e, k=k, b=b: e.matmul(ps_bank, self.ones_bf[:], sq[b], start=(k == 0), stop=(k == KC - 1)),
                     reads=[sqR[b]], writes=[psR], signal=True)
            P.op(P.dve, lambda e: e.tensor_scalar(out=rstd[:, ts], in0=ps_bank, scalar1=1.0 / D, scalar2=EPS,
                                                  op0=ALU.mult, op1=ALU.add),
                 reads=[psR], writes=[rstdR[t]])
            P.op(P.act, lambda e: e.activation(out=rstd[:, ts], in_=rstd[:, ts], func=AF.Sqrt),
                 reads=[rstdR[t]], writes=[rstdR[t]])
            P.op(P.dve, lambda e: e.reciprocal(out=rstd[:, ts], in_=rstd[:, ts]),
                 reads=[rstdR[t]], writes=[rstdR[t]])

    def pool_layer(self, li):
        P, nc = self.P, self.nc
        PADW = S + 16
        with ExitStack() as es:
            rstd = es.enter_context(nc.sbuf_tensor(f"{self.uq()}pl_rstd", [128, S], F32))
            sq = [es.enter_context(nc.sbuf_tensor(f"{self.uq()}pl_sq{i}", [128, 512], BF16)) for i in range(2)]
            X = es.enter_context(nc.sbuf_tensor(f"{self.uq()}pl_xp", [128, PADW], F32))
            A = es.enter_context(nc.sbuf_tensor(f"{self.uq()}pl_sa", [128, PADW], F32))
            B = es.enter_context(nc.sbuf_tensor(f"{self.uq()}pl_sb", [128, PADW], F32))
            tmpE = es.enter_context(nc.sbuf_tensor(f"{self.uq()}pl_tmpe", [128, 16], F32))
            pooled = es.enter_context(nc.sbuf_tensor(f"{self.uq()}pl_pooled", [128, KC, S], BF16))
            invc = es.enter_context(nc.sbuf_tensor(f"{self.uq()}pl_invc", [128, 4, 16], F32))
            ps_stat = es.enter_context(nc.psum_tensor(f"{self.uq()}pl_ps_stat", [128, 512], F32))
            ps_y = [es.enter_context(nc.psum_tensor(f"{self.uq()}pl_ps_y{i}", [128, 512], F32)) for i in range(2)]
            rstdR = [Res(f"rstd{t}") for t in range(4)]
            sqR = [Res("sq0"), Res("sq1")]
            XR, AR, BR, ER = Res("xp"), Res("sa"), Res("sb"), Res("tmpe")
            pooledR = [Res(f"pooled{k}") for k in range(KC)]
            invcR = Res("invc")
            psR = Res("ps_stat")
            psyR = [Res("psy0"), Res("psy1")]

            P.dma(P.sp, invc[:], self.dram["invcnt"], writes=[invcR])
            self.rms_stats(rstd, rstdR, ps_stat[:], psR, [s_[:] for s_ in sq], sqR)
            P.op(P.dve, lambda e: e.memset(X[:], 0.0), writes=[XR])
            for k in range(KC):
                g = k // 2
                win = (2, 4, 8, 16)[g]
                P.op(P.dve, lambda e: e.scalar_tensor_tensor(
                    out=X[:, 8:8 + S], in0=self.h[:, k, :], scalar=self.par("mix_norm", li * KC + k),
                    in1=rstd[:], op0=ALU.mult, op1=ALU.mult),
                    reads=[self.hR[k][t] for t in range(4)] + rstdR, writes=[XR])
                P.op(P.dve, lambda e: e.tensor_tensor(out=A[:, 1:PADW], in0=X[:, 0:PADW - 1], in1=X[:, 1:PADW], op=ALU.add),
                     reads=[XR], writes=[AR])
                cur, curR, oth, othR = A, AR, B, BR
                lo, hi, sh = 1, PADW, 1
                for lvl in range(g):
                    a0, a1 = lo + sh, hi - sh
                    P.op(P.dve, lambda e: e.tensor_tensor(
                        out=oth[:, a0:a1], in0=cur[:, a0 - sh:a1 - sh], in1=cur[:, a0 + sh:a1 + sh], op=ALU.add),
                        reads=[curR], writes=[othR])
                    lo, hi = a0, a1
                    cur, curR, oth, othR = oth, othR, cur, curR
                    sh *= 2
                assert lo <= 8 and hi >= 8 + S
                P.op(P.dve, lambda e: e.scalar_tensor_tensor(
                    out=pooled[:, k, :], in0=cur[:, 8:8 + S], scalar=1.0 / win, in1=X[:, 8:8 + S],
                    op0=ALU.mult, op1=ALU.subtract),
                    reads=[curR, XR], writes=[pooledR[k]])
                for (c0, e0) in ((0, 0), (S - 8, 8)):
                    P.op(P.dve, lambda e: e.tensor_tensor(
                        out=tmpE[:, e0:e0 + 8], in0=cur[:, 8 + c0:16 + c0], in1=invc[:, g, e0:e0 + 8], op=ALU.mult),
                        reads=[curR, invcR], writes=[ER])
                    P.op(P.dve, lambda e: e.tensor_tensor(
                        out=pooled[:, k, c0:c0 + 8], in0=tmpE[:, e0:e0 + 8], in1=X[:, 8 + c0:16 + c0], op=ALU.subtract),
                        reads=[ER, XR, pooledR[k]], writes=[pooledR[k]])
            u = 0
            for g in range(4):
                wv, wR = self.ws.get(("pool", g))
                wv = wv.rearrange("p (kc e) -> p kc e", kc=2)
                for ec in range(2):
                    m = 2 * g + ec
                    for t in range(4):
                        ts = slice(t * 512, (t + 1) * 512)
                        pb = u % 2
                        u += 1
                        P.mm_group(ps_y[pb][:], [(wv[:, kc, ec * 128:(ec + 1) * 128], pooled[:, 2 * g + kc, ts]) for kc in range(2)],
                                   reads=[wR, pooledR[2 * g], pooledR[2 * g + 1]], writes=[psyR[pb]])
                        P.op(P.dve, lambda e: e.scalar_tensor_tensor(
                            out=self.h[:, m, ts], in0=ps_y[pb][:], scalar=self.par("pool_scale", m),
                            in1=self.h[:, m, ts], op0=ALU.mult, op1=ALU.add),
                            reads=[psyR[pb], self.hR[m][t]], writes=[self.hR[m][t]])
                self.ws.release()
            P.barrier()

    def ffn_layer(self, li):
        P, nc = self.P, self.nc
        HT = 1024
        with ExitStack() as es:
            rstd = es.enter_context(nc.sbuf_tensor(f"{self.uq()}ff_rstd", [128, S], F32))
            sq = [es.enter_context(nc.sbuf_tensor(f"{self.uq()}ff_sq{i}", [128, 512], BF16)) for i in range(2)]
            hn = es.enter_context(nc.sbuf_tensor(f"{self.uq()}ff_hn", [128, KC, HT + 2], BF16))
            act = es.enter_context(nc.sbuf_tensor(f"{self.uq()}ff_act", [128, FC, HT], BF16))
            hn_halo = es.enter_context(nc.sbuf_tensor(f"{self.uq()}ff_hnhalo", [128, KC, 1], BF16))
            haloR = Res("hn_halo")
            gext = [es.enter_context(nc.sbuf_tensor(f"{self.uq()}ff_gext{i}", [128, HT + 2], F32)) for i in range(2)]
            cbuf = [es.enter_context(nc.sbuf_tensor(f"{self.uq()}ff_c{i}", [128, HT], F32)) for i in range(2)]
            ps_g = [es.enter_context(nc.psum_tensor(f"{self.uq()}ff_psg{i}", [128, 512], F32)) for i in range(2)]
            ps_u = [es.enter_context(nc.psum_tensor(f"{self.uq()}ff_psu{i}", [128, 512], F32)) for i in range(2)]
            ps_o = [es.enter_context(nc.psum_tensor(f"{self.uq()}ff_pso{i}", [128, 512], F32)) for i in range(2)]
            ps_h = es.enter_context(nc.psum_tensor(f"{self.uq()}ff_psh", [128, 512], F32))
            ps_stat = es.enter_context(nc.psum_tensor(f"{self.uq()}ff_psstat", [128, 512], F32))
            rstdR = [Res(f"rstd{t}") for t in range(4)]
            sqR = [Res("sq0"), Res("sq1")]
            hnR = Res("hn")
            actR = [Res(f"act{f}") for f in range(FC)]
            gextR = [Res("gext0"), Res("gext1")]
            cR = [Res("c0"), Res("c1")]
            psgR = [Res("psg0"), Res("psg1")]
            psuR = [Res("psu0"), Res("psu1")]
            psoR = [Res("pso0"), Res("pso1")]
            pshR = Res("psh")
            psR = Res("ps_stat")

            self.rms_stats(rstd, rstdR, ps_stat[:], psR, [s_[:] for s_ in sq], sqR)
            unit = 0
            ou = 0
            for hf in range(2):
                t0 = hf * HT
                if hf == 0:
                    c0, c1, ta, tb = 1, HT + 2, 0, HT + 1
                    P.op(P.dve, lambda e: e.memset(hn[:, :, 0:1], 0.0), writes=[hnR])
                else:
                    c0, c1, ta, tb = 1, HT + 1, t0, S
                    P.op(P.dve, lambda e: e.memset(hn[:, :, HT + 1:HT + 2], 0.0), writes=[hnR])
                    P.op(P.dve, lambda e: e.tensor_copy(out=hn[:, :, 0:1], in_=hn_halo[:]), reads=[haloR], writes=[hnR])
                for k in range(KC):
                    P.op(P.dve, lambda e: e.scalar_tensor_tensor(
                        out=hn[:, k, c0:c1], in0=self.h[:, k, ta:tb], scalar=self.par("ffn_norm", li * KC + k),
                        in1=rstd[:, ta:tb], op0=ALU.mult, op1=ALU.mult),
                        reads=[self.hR[k][t] for t in range(4)] + rstdR, writes=[hnR])
                if hf == 0:
                    P.op(P.dve, lambda e: e.tensor_copy(out=hn_halo[:], in_=hn[:, :, HT:HT + 1]), reads=[hnR], writes=[haloR])
                for f in range(FC):
                    wv, wR = self.ws.get(("up", li, hf, f))
                    wv = wv.rearrange("p (two k c) -> p two k c", two=2, k=KC)
                    fb = f % 2
                    for s in range(2):
                        ub = unit % 2
                        unit += 1
                        cs = slice(1 + s * 512, 1 + (s + 1) * 512)
                        P.mm_group(ps_g[ub][:], [(wv[:, 0, k, :], hn[:, k, cs]) for k in range(KC)],
                                   reads=[wR, hnR], writes=[psgR[ub]])
                        P.mm_group(ps_u[ub][:], [(wv[:, 1, k, :], hn[:, k, cs]) for k in range(KC)],
                                   reads=[wR, hnR], writes=[psuR[ub]])
                        P.op(P.act, lambda e, ub=ub, fb=fb, cs=cs: e.activation(out=gext[fb][:, cs], in_=ps_g[ub][:], func=AF.Identity),
                             reads=[psgR[ub]], writes=[gextR[fb]])
                        if s == 0:
                            self._ub0 = ub
                    P.mm_group(ps_h[:, 0:2], [(wv[:, 0, k, :], hn[:, k, 0:HT + 2:HT + 1]) for k in range(KC)],
                               reads=[wR, hnR], writes=[pshR])
                    self.ws.release()
                    P.op(P.act, lambda e, fb=fb: e.activation(out=gext[fb][:, 0:HT + 2:HT + 1], in_=ps_h[:, 0:2], func=AF.Identity),
                         reads=[pshR], writes=[gextR[fb]])
                    cw = lambda j: self.par("ffn_conv_w", (li * 3 + j) * FC + f)
                    cb = self.par("ffn_conv_b", li * FC + f)
                    G, C = gext[fb], cbuf[fb]
                    P.op(P.dve, lambda e, G=G, C=C, cw=cw, cb=cb: e.tensor_scalar(
                        out=C[:], in0=G[:, 1:HT + 1], scalar1=cw(1), scalar2=cb, op0=ALU.mult, op1=ALU.add),
                        reads=[gextR[fb]], writes=[cR[fb]])
                    P.op(P.dve, lambda e, G=G, C=C, cw=cw: e.scalar_tensor_tensor(
                        out=C[:], in0=G[:, 0:HT], scalar=cw(0), in1=C[:], op0=ALU.mult, op1=ALU.add),
                        reads=[gextR[fb], cR[fb]], writes=[cR[fb]])
                    P.op(P.dve, lambda e, G=G, C=C, cw=cw: e.scalar_tensor_tensor(
                        out=C[:], in0=G[:, 2:HT + 2], scalar=cw(2), in1=C[:], op0=ALU.mult, op1=ALU.add),
                        reads=[gextR[fb], cR[fb]], writes=[cR[fb]])
                    P.op(P.act, lambda e, C=C: e.activation(out=C[:], in_=C[:], func=AF.Gelu),
                         reads=[cR[fb]], writes=[cR[fb]])
                    for s in range(2):
                        ub = (unit - 2 + s) % 2
                        P.op(P.dve, lambda e, C=C, s=s, ub=ub, f=f: e.tensor_tensor(
                            out=act[:, f, s * 512:(s + 1) * 512], in0=C[:, s * 512:(s + 1) * 512], in1=ps_u[ub][:], op=ALU.mult),
                            reads=[cR[fb], psuR[ub]], writes=[actR[f]])
                for m in range(KC):
                    wv, wR = self.ws.get(("dn", li, hf, m))
                    wv = wv.rearrange("p (j c) -> p j c", j=FC)
                    for s in range(2):
                        ob = ou % 2
                        ou += 1
                        t = hf * 2 + s
                        ts = slice(t * 512, (t + 1) * 512)
                        P.mm_group(ps_o[ob][:], [(wv[:, j, :], act[:, j, s * 512:(s + 1) * 512]) for j in range(FC)],
                                   reads=[wR] + actR, writes=[psoR[ob]])
                        P.op(P.dve, lambda e, m=m, ts=ts, ob=ob: e.tensor_tensor(
                            out=self.h[:, m, ts], in0=self.h[:, m, ts], in1=ps_o[ob][:], op=ALU.add),
                            reads=[psoR[ob], self.hR[m][t]], writes=[self.hR[m][t]])
                    self.ws.release()
            P.barrier()

    def rope_tables(self, es):
        P, nc = self.P, self.nc
        cos = es.enter_context(nc.sbuf_tensor(f"{self.uq()}cos", [128, S], F32))
        sin = es.enter_context(nc.sbuf_tensor(f"{self.uq()}sin", [128, S], F32))
        tabR = Res("ropetab")
        TWO_PI = 2.0 * math.pi
        with ExitStack() as e2:
            posi = e2.enter_context(nc.sbuf_tensor(f"{self.uq()}posi", [128, S], I32))
            ang = e2.enter_context(nc.sbuf_tensor(f"{self.uq()}ang", [128, S], F32))
            kf = e2.enter_context(nc.sbuf_tensor(f"{self.uq()}kf", [128, S], F32))
            ki = e2.enter_context(nc.sbuf_tensor(f"{self.uq()}ki", [128, S], I32))
            pR, aR, kfR, kiR = Res("posi"), Res("ang"), Res("kf"), Res("ki")
            P.dma(P.sp, posi[:], self.dram["pos_b"], writes=[pR])
            P.op(P.dve, lambda e: e.tensor_copy(out=ang[:], in_=posi[:]), reads=[pR], writes=[aR])
            P.op(P.dve, lambda e: e.tensor_scalar(out=ang[:], in0=ang[:], scalar1=self.par("inv_freq", 0), scalar2=None,
                                                  op0=ALU.mult), reads=[aR], writes=[aR])
            for tab, shift, use_sgn in ((sin, 0.0, True), (cos, math.pi / 2.0, False)):
                P.op(P.dve, lambda e: e.tensor_scalar(out=ki[:], in0=ang[:], scalar1=shift, scalar2=1.0 / TWO_PI,
                                                      op0=ALU.add, op1=ALU.mult), reads=[aR], writes=[kiR])
                P.op(P.dve, lambda e: e.tensor_copy(out=kf[:], in_=ki[:]), reads=[kiR], writes=[kfR])
                P.op(P.dve, lambda e: e.scalar_tensor_tensor(out=kf[:], in0=kf[:], scalar=-TWO_PI, in1=ang[:],
                                                             op0=ALU.mult, op1=ALU.add), reads=[kfR, aR], writes=[kfR])
                P.op(P.dve, lambda e: e.tensor_scalar(out=kf[:], in0=kf[:], scalar1=shift, scalar2=3.141592,
                                                      op0=ALU.add, op1=ALU.min), reads=[kfR], writes=[kfR])
                P.op(P.dve, lambda e: e.tensor_scalar(out=kf[:], in0=kf[:], scalar1=-3.141592, scalar2=None,
                                                      op0=ALU.max), reads=[kfR], writes=[kfR])
                if use_sgn:
                    P.op(P.act, lambda e: e.activation(out=tab[:], in_=kf[:], func=AF.Sin, scale=self.par("rope_sgn", 0)),
                         reads=[kfR], writes=[tabR])
                else:
                    P.op(P.act, lambda e: e.activation(out=tab[:], in_=kf[:], func=AF.Sin),
                         reads=[kfR], writes=[tabR])
            P.barrier()
        return cos, sin, tabR

    def rope_apply(self, ps, psR, dst, dstR, cs_, sn_, tabR, qr, qrR, qs, qsR, t1, t1R, view=None):
        P = self.P
        mask = list(range(16, 32)) + list(range(0, 16))
        v = (lambda a: a) if view is None else view
        P.op(P.act, lambda e: e.activation(out=qr, in_=ps, func=AF.Identity), reads=[psR], writes=[qrR])
        P.op(P.dve, lambda e: e.stream_shuffle(out=qs, in_=qr, mask=mask), reads=[qrR], writes=[qsR])
        P.op(P.dve, lambda e: e.tensor_tensor(out=v(t1), in0=v(qr), in1=cs_, op=ALU.mult), reads=[qrR, tabR], writes=[t1R])
        P.op(P.dve, lambda e: e.tensor_tensor(out=v(qs), in0=v(qs), in1=sn_, op=ALU.mult), reads=[qsR, tabR], writes=[qsR])
        P.op(P.dve, lambda e: e.tensor_tensor(out=dst, in0=t1, in1=qs, op=ALU.add), reads=[t1R, qsR], writes=[dstR])

    def diff_layer(self, li):
        P, nc = self.P, self.nc
        G = 2
        NG = 8 // G
        lam_init = 0.8 - 0.6 * math.exp(-0.3 * li)
        with ExitStack() as es:
            cos, sin, tabR = self.rope_tables(es)
            hn = es.enter_context(nc.sbuf_tensor(f"{self.uq()}da_hn", [128, KC, S], BF16))
            bank = {i: es.enter_context(nc.psum_tensor(f"{self.uq()}da_ps{i}", [128, 512], F32)) for i in (0, 1, 6, 7)}
            bR = {i: Res(f"bank{i}") for i in (0, 1, 6, 7)}
            rstdR = [Res(f"rstd{t}") for t in range(4)]
            sqR = [Res("sq0"), Res("sq1")]
            hnR = Res("hn")
            with ExitStack() as e2:
                rstd = e2.enter_context(nc.sbuf_tensor(f"{self.uq()}da_rstd", [128, S], F32))
                sq = [e2.enter_context(nc.sbuf_tensor(f"{self.uq()}da_sq{i}", [128, 512], BF16)) for i in range(2)]
                self.rms_stats(rstd, rstdR, bank[7][:], bR[7], [s_[:] for s_ in sq], sqR)
                for k in range(KC):
                    P.op(P.dve, lambda e: e.scalar_tensor_tensor(
                        out=hn[:, k, :], in0=self.h[:, k, :], scalar=self.par("mix_norm", li * KC + k),
                        in1=rstd[:], op0=ALU.mult, op1=ALU.mult),
                        reads=[self.hR[k][t] for t in range(4)] + rstdR, writes=[hnR])
                P.barrier()

            qg = es.enter_context(nc.sbuf_tensor(f"{self.uq()}da_q", [128, G, S], BF16))
            kg = es.enter_context(nc.sbuf_tensor(f"{self.uq()}da_k", [128, G, S], BF16))
            vaug = es.enter_context(nc.sbuf_tensor(f"{self.uq()}da_v", [128, 16, G, 130], BF16))
            oT = es.enter_context(nc.sbuf_tensor(f"{self.uq()}da_oT", [128, G, S], BF16))
            pT = [es.enter_context(nc.sbuf_tensor(f"{self.uq()}da_pT{i}", [128, 512], BF16)) for i in range(3)]
            qs = [es.enter_context(nc.sbuf_tensor(f"{self.uq()}da_qs{i}", [128, 512], F32)) for i in range(2)]
            t1 = [es.enter_context(nc.sbuf_tensor(f"{self.uq()}da_t1{i}", [128, 512], F32)) for i in range(2)]
            qr = [es.enter_context(nc.sbuf_tensor(f"{self.uq()}da_qr{i}", [128, 512], F32)) for i in range(2)]
            qrR = [Res("qr0"), Res("qr1")]
            tmp = es.enter_context(nc.sbuf_tensor(f"{self.uq()}da_tmp", [128, 4, 128], F32))
            tmpR = Res("tmp")
            OA = es.enter_context(nc.psum_tensor(f"{self.uq()}da_oa", [128, 4, 512], F32))
            oaR = Res("oa")
            o1 = es.enter_context(nc.sbuf_tensor(f"{self.uq()}da_o1", [128, 4, 128], F32))
            sm = es.enter_context(nc.sbuf_tensor(f"{self.uq()}da_sm", [128, 4, 8], F32))
            onb = es.enter_context(nc.sbuf_tensor(f"{self.uq()}da_onb", [128, 4, 128], BF16))
            lamt = es.enter_context(nc.sbuf_tensor(f"{self.uq()}da_lam", [128, 8], F32))
            lprod = es.enter_context(nc.sbuf_tensor(f"{self.uq()}da_lprod", [128, 2, 64], F32))
            subl = es.enter_context(nc.sbuf_tensor(f"{self.uq()}da_subl", [128, 128], F32))
            qR = [Res(f"q{i}") for i in range(G)]
            kR = [Res(f"k{i}") for i in range(G)]
            vR = [Res(f"v{i}") for i in range(G)]
            oTR = [Res(f"oT{i}") for i in range(G)]
            pTR = [Res(f"pT{i}") for i in range(3)]
            qsR = [Res("qs0"), Res("qs1")]
            t1R = [Res("t10"), Res("t11")]
            o1R = Res("o1")
            smR = Res("sm")
            onbR = Res("onb")
            pending = []
            lamR = Res("lam")
            sublR = Res("subl")

            P.op(P.dve, lambda e: e.tensor_tensor(out=lprod[:, 0, :], in0=self.par("lam_q1", 0, 64), in1=self.par("lam_k1", 0, 64), op=ALU.mult),
                 writes=[lamR])
            P.op(P.dve, lambda e: e.tensor_tensor(out=lprod[:, 1, :], in0=self.par("lam_q2", 0, 64), in1=self.par("lam_k2", 0, 64), op=ALU.mult),
                 writes=[lamR])
            P.op(P.dve, lambda e: e.reduce_sum(out=lamt[:, 0:1], in_=lprod[:, 0, :], axis=AX.X), reads=[lamR], writes=[lamR])
            P.op(P.dve, lambda e: e.reduce_sum(out=lamt[:, 1:2], in_=lprod[:, 1, :], axis=AX.X), reads=[lamR], writes=[lamR])
            P.op(P.act, lambda e: e.activation(out=lamt[:, 2:4], in_=lamt[:, 0:2], func=AF.Exp), reads=[lamR], writes=[lamR])
            P.op(P.dve, lambda e: e.tensor_tensor(out=lamt[:, 4:5], in0=lamt[:, 3:4], in1=lamt[:, 2:3], op=ALU.subtract),
                 reads=[lamR], writes=[lamR])
            P.op(P.dve, lambda e: e.tensor_scalar(out=lamt[:, 5:6], in0=lamt[:, 4:5], scalar1=-lam_init, scalar2=None, op0=ALU.add),
                 reads=[lamR], writes=[lamR])
            neg_lam = lamt[:, 5:6]
            P.op(P.dve, lambda e: e.tensor_scalar(out=subl[:], in0=self.par("subln", 0, 128), scalar1=1.0 - lam_init, scalar2=None, op0=ALU.mult),
                 writes=[sublR])
            P.op(P.dve, lambda e: e.memset(vaug[:, :, :, 128:130], 1.0), writes=vR)

            stage = getattr(self, "dbg_stage", 99)
            pu = 0
            ru = 0
            si = 0
            tu = 0
            for gi in range(NG):
                for which, dst, dR in (("q", qg, qR), ("k", kg, kR)):
                    for hl in range(G):
                        wv, wR = self.ws.get(("d" + which, gi, hl))
                        wv = wv.rearrange("p (k c) -> p k c", k=KC)
                        for t in range(4):
                            ts = slice(t * 512, (t + 1) * 512)
                            b = pu % 2
                            pu += 1
                            P.mm_group(bank[b][:], [(wv[:, k, :], hn[:, k, ts]) for k in range(KC)],
                                       reads=[wR, hnR], writes=[bR[b]])
                            r = ru % 2
                            ru += 1
                            self.rope_apply(bank[b][:], bR[b], dst[:, hl, ts], dR[hl], cos[:, ts], sin[:, ts], tabR,
                                            qr[r][:], qrR[r], qs[r][:], qsR[r], t1[r][:], t1R[r])
                        self.ws.release()
                wv, wR = self.ws.get(("dv", gi))
                wv = wv.rearrange("p (k c) -> p k c", k=KC)
                for blk in range(16):
                    b = pu % 2
                    pu += 1
                    bs = slice(blk * 128, (blk + 1) * 128)
                    P.mm_group(bank[b][:, 0:256], [(hn[:, k, bs], wv[:, k, :]) for k in range(KC)],
                               reads=[wR, hnR], writes=[bR[b]])
                    P.op(P.act, lambda e: e.activation(out=vaug[:, blk, :, 0:128],
                                                       in_=bank[b][:, 0:256].rearrange("p (g c) -> p g c", g=G), func=AF.Identity),
                         reads=[bR[b]], writes=vR)
                self.ws.release()
                for hl in range(G):
                    if stage < 3:
                        break
                    for qt in range(4):
                        qsl = slice(qt * 512, (qt + 1) * 512)
                        for c in range(2):
                            ps_ = slice(c * 64, (c + 1) * 64)
                            for kb in range(16):
                                sb_ = si % 2
                                pb = si % 3
                                si += 1
                                P.op(P.pe, lambda e: e.matmul(bank[sb_][:], kg[ps_, hl, kb * 128:(kb + 1) * 128], qg[ps_, hl, qsl],
                                                              start=True, stop=True),
                                     reads=[kR[hl], qR[hl]], writes=[bR[sb_]])
                                P.op(P.act, lambda e: e.activation(out=pT[pb][:], in_=bank[sb_][:], func=AF.Exp, scale=0.125),
                                     reads=[bR[sb_]], writes=[pTR[pb]])
                                for qb in range(4):
                                    P.op(P.pe, lambda e: e.matmul(OA[:, qb, 0:130], pT[pb][:, qb * 128:(qb + 1) * 128],
                                                                  vaug[:, kb, hl, :], start=(kb == 0), stop=(kb == 15)),
                                         reads=[pTR[pb], vR[hl]], writes=[oaR], signal=(qb == 3))
                            if c == 0:
                                for fn in pending:
                                    fn()
                                pending.clear()
                            bc = lambda ap: ap.broadcast_to([128, 4, 128])
                            if c == 0:
                                P.op(P.dve, lambda e: e.reciprocal(out=sm[:, :, 0:1], in_=OA[:, :, 128:129]), reads=[oaR], writes=[smR])
                                P.op(P.dve, lambda e: e.tensor_tensor(out=o1[:], in0=OA[:, :, 0:128], in1=bc(sm[:, :, 0:1]), op=ALU.mult),
                                     reads=[oaR, smR], writes=[o1R])
                            else:
                                P.op(P.dve, lambda e: e.reciprocal(out=sm[:, :, 1:2], in_=OA[:, :, 128:129]), reads=[oaR], writes=[smR])
                                P.op(P.dve, lambda e: e.tensor_scalar(out=sm[:, :, 2:3], in0=sm[:, :, 1:2], scalar1=neg_lam, scalar2=None,
                                                                      op0=ALU.mult), reads=[smR, lamR], writes=[smR])
                                P.op(P.dve, lambda e: e.tensor_tensor(out=tmp[:], in0=OA[:, :, 0:128], in1=bc(sm[:, :, 2:3]), op=ALU.mult),
                                     reads=[oaR, smR], writes=[tmpR])
                                P.op(P.dve, lambda e: e.tensor_tensor(out=o1[:], in0=o1[:], in1=tmp[:], op=ALU.add),
                                     reads=[o1R, tmpR], writes=[o1R])
                                P.op(P.dve, lambda e: e.tensor_tensor(out=tmp[:], in0=o1[:], in1=o1[:], op=ALU.mult),
                                     reads=[o1R], writes=[tmpR])
                                P.op(P.dve, lambda e: e.reduce_sum(out=sm[:, :, 3], in_=tmp[:], axis=AX.X), reads=[tmpR], writes=[smR])
                                P.op(P.dve, lambda e: e.tensor_scalar(out=sm[:, :, 4:5], in0=sm[:, :, 3:4], scalar1=1.0 / 128.0, scalar2=1e-5,
                                                                      op0=ALU.mult, op1=ALU.add), reads=[smR], writes=[smR])
                                P.op(P.act, lambda e: e.activation(out=sm[:, :, 5:6], in_=sm[:, :, 4:5], func=AF.Sqrt), reads=[smR], writes=[smR])
                                P.op(P.dve, lambda e: e.reciprocal(out=sm[:, :, 6:7], in_=sm[:, :, 5:6]), reads=[smR], writes=[smR])
                                P.op(P.dve, lambda e: e.tensor_tensor(out=tmp[:], in0=o1[:], in1=bc(sm[:, :, 6:7]), op=ALU.mult),
                                     reads=[o1R, smR], writes=[tmpR])
                                P.op(P.dve, lambda e: e.tensor_tensor(out=onb[:], in0=tmp[:], in1=bc(subl[:].unsqueeze(1)), op=ALU.mult),
                                     reads=[tmpR, sublR], writes=[onbR])
                                for qb in range(4):
                                    def tr(qb=qb, hl=hl, q0=qt * 512 + qb * 128):
                                        nonlocal tu
                                        ib = tu % 2
                                        tu += 1
                                        tps = bank[6 + ib][:].bitcast(BF16)
                                        P.op(P.pe, lambda e: e.transpose(tps[:, 0:128], onb[:, qb, :], self.ident_bf[:]),
                                             reads=[onbR], writes=[bR[6 + ib]])
                                        P.op(P.act, lambda e: e.activation(out=oT[:, hl, q0:q0 + 128], in_=tps[:, 0:128], func=AF.Identity),
                                             reads=[bR[6 + ib]], writes=[oTR[hl]])
                                    pending.append(tr)
                for fn in pending:
                    fn()
                pending.clear()
                if stage < 3:
                    P.op(P.dve, lambda e: e.memset(oT[:], 0.0), writes=oTR)
                wv, wR = self.ws.get(("do", gi))
                wv = wv.rearrange("p (kc c) -> p kc c", kc=G)
                for m in range(KC):
                    for t in range(4):
                        ts = slice(t * 512, (t + 1) * 512)
                        b = pu % 2
                        pu += 1
                        P.mm_group(bank[b][:], [(wv[:, kc, m * 128:(m + 1) * 128], oT[:, kc, ts]) for kc in range(G)],
                                   reads=[wR] + oTR, writes=[bR[b]])
                        P.op(P.dve, lambda e: e.tensor_tensor(out=self.h[:, m, ts], in0=self.h[:, m, ts], in1=bank[b][:], op=ALU.add),
                             reads=[bR[b], self.hR[m][t]], writes=[self.hR[m][t]])
                self.ws.release()
            P.barrier()

    def lru_layer(self, li):
        P, nc = self.P, self.nc
        QT = 512
        with ExitStack() as es:
            hn = es.enter_context(nc.sbuf_tensor(f"{self.uq()}lr_hn", [128, KC, S], BF16))
            y = es.enter_context(nc.sbuf_tensor(f"{self.uq()}lr_y", [128, KC, S], BF16))
            bank = [es.enter_context(nc.psum_tensor(f"{self.uq()}lr_ps{i}", [128, 512], F32)) for i in range(8)]
            bR = [Res(f"bank{i}") for i in range(8)]
            hnR = Res("hn")
            yR = [Res(f"y{k}") for k in range(KC)]
            with ExitStack() as e2:
                rstd = e2.enter_context(nc.sbuf_tensor(f"{self.uq()}lr_rstd", [128, S], F32))
                sq = [e2.enter_context(nc.sbuf_tensor(f"{self.uq()}lr_sq{i}", [128, 512], BF16)) for i in range(2)]
                rstdR = [Res(f"rstd{t}") for t in range(4)]
                sqR = [Res("sq0"), Res("sq1")]
                self.rms_stats(rstd, rstdR, bank[7][:], bR[7], [s_[:] for s_ in sq], sqR)
                for k in range(KC):
                    P.op(P.dve, lambda e: e.scalar_tensor_tensor(
                        out=hn[:, k, :], in0=self.h[:, k, :], scalar=self.par("mix_norm", li * KC + k),
                        in1=rstd[:], op0=ALU.mult, op1=ALU.mult),
                        reads=[self.hR[k][t] for t in range(4)] + rstdR, writes=[hnR])
                P.barrier()
            usb = es.enter_context(nc.sbuf_tensor(f"{self.uq()}lr_u", [128, S + 6], F32))
            gg = es.enter_context(nc.sbuf_tensor(f"{self.uq()}lr_gg", [128, S], BF16))
            hs = es.enter_context(nc.sbuf_tensor(f"{self.uq()}lr_hs", [128, S], F32))
            hb = es.enter_context(nc.sbuf_tensor(f"{self.uq()}lr_hb", [128, S], F32))
            xc = [es.enter_context(nc.sbuf_tensor(f"{self.uq()}lr_xc{i}", [128, QT], F32)) for i in range(2)]
            xcb = [es.enter_context(nc.sbuf_tensor(f"{self.uq()}lr_xcb{i}", [128, QT], BF16)) for i in range(2)]
            ra = [es.enter_context(nc.sbuf_tensor(f"{self.uq()}lr_ra{i}", [128, QT], F32)) for i in range(2)]
            ix = [es.enter_context(nc.sbuf_tensor(f"{self.uq()}lr_ix{i}", [128, QT], F32)) for i in range(2)]
            mm_ = [es.enter_context(nc.sbuf_tensor(f"{self.uq()}lr_m{i}", [128, QT], F32)) for i in range(2)]
            ca = es.enter_context(nc.sbuf_tensor(f"{self.uq()}lr_ca", [128, 16], F32))
            usbR, ggR, hsR, hbR = Res("usb"), Res("gg"), Res("hs"), Res("hb")
            xcR = [Res("xc0"), Res("xc1")]
            xcbR = [Res("xcb0"), Res("xcb1")]
            raR = [Res("ra0"), Res("ra1")]
            ixR = [Res("ix0"), Res("ix1")]
            mR = [Res("m0"), Res("m1")]
            caR = Res("ca")

            P.op(P.act, lambda e: e.activation(out=ca[:], in_=self.par("lru_lambda", 0, 16), func=AF.Exp, scale=-1.0), writes=[caR])
            P.op(P.act, lambda e: e.activation(out=ca[:], in_=ca[:], func=AF.Ln, bias=1.0), reads=[caR], writes=[caR])
            P.op(P.dve, lambda e: e.tensor_scalar(out=ca[:], in0=ca[:], scalar1=-8.0, scalar2=None, op0=ALU.mult), reads=[caR], writes=[caR])
            P.op(P.dve, lambda e: e.memset(usb[:], 0.0), writes=[usbR])

            pu = 0
            gu = 0
            eu = 0
            for c in range(KC):
                wv, wR = self.ws.get(("lin", c))
                wv = wv.rearrange("p (two k c) -> p two k c", k=KC, two=2)
                bdv, bdR = self.ws.get(("lbd", c))
                bdv = bdv.rearrange("p (d ax c) -> p d ax c", d=2, ax=2)
                for t in range(4):
                    ts = slice(t * 512, (t + 1) * 512)
                    b = pu % 2
                    pu += 1
                    P.mm_group(bank[b][:], [(wv[:, 0, k, :], hn[:, k, ts]) for k in range(KC)], reads=[wR, hnR], writes=[bR[b]])
                    P.op(P.act, lambda e: e.activation(out=gg[:, ts], in_=bank[b][:], func=AF.Gelu_apprx_tanh),
                         reads=[bR[b]], writes=[ggR])
                    b = pu % 2
                    pu += 1
                    P.mm_group(bank[b][:], [(wv[:, 1, k, :], hn[:, k, ts]) for k in range(KC)], reads=[wR, hnR], writes=[bR[b]])
                    P.op(P.act, lambda e: e.activation(out=usb[:, 3 + t * 512:3 + (t + 1) * 512], in_=bank[b][:], func=AF.Identity),
                         reads=[bR[b]], writes=[usbR])
                for d in range(2):
                    order = (0, 1, 2, 3) if d == 0 else (3, 2, 1, 0)
                    hd, hdR = (hs, hsR) if d == 0 else (hb, hbR)
                    pc = (d * 4) * KC + c
                    for qi, q in enumerate(order):
                        a = q * QT
                        e_ = eu % 2
                        eu += 1
                        X, XB, RA, IX, M = xc[e_], xcb[e_], ra[e_], ix[e_], mm_[e_]
                        cw = lambda j: self.par("lru_conv_w", (d * 4 + j) * KC + c)
                        cb = self.par("lru_conv_b", d * KC + c)
                        if d == 0:
                            off = lambda j: a + j
                        else:
                            off = lambda j: a + 6 - j
                        P.op(P.dve, lambda e: e.tensor_scalar(out=X[:], in0=usb[:, off(0):off(0) + QT], scalar1=cw(0), scalar2=cb,
                                                              op0=ALU.mult, op1=ALU.add), reads=[usbR], writes=[xcR[e_]])
                        for j in range(1, 4):
                            P.op(P.dve, lambda e: e.scalar_tensor_tensor(out=X[:], in0=usb[:, off(j):off(j) + QT], scalar=cw(j), in1=X[:],
                                                                         op0=ALU.mult, op1=ALU.add),
                                 reads=[usbR, xcR[e_]], writes=[xcR[e_]])
                        P.op(P.act, lambda e: e.activation(out=XB[:], in_=X[:], func=AF.Identity), reads=[xcR[e_]], writes=[xcbR[e_]])
                        gb = gu % 2
                        gu += 1
                        rb, ib = bank[2 + gb], bank[4 + gb]
                        P.op(P.pe, lambda e: e.matmul(rb[:], bdv[:, d, 0, :], XB[:], start=True, stop=True),
                             reads=[bdR, xcbR[e_]], writes=[bR[2 + gb]])
                        P.op(P.pe, lambda e: e.matmul(ib[:], bdv[:, d, 1, :], XB[:], start=True, stop=True),
                             reads=[bdR, xcbR[e_]], writes=[bR[4 + gb]])
                        P.op(P.act, lambda e: e.activation(out=RA[:], in_=rb[:], func=AF.Sigmoid, bias=self.par("lru_b_a", d * KC + c)),
                             reads=[bR[2 + gb]], writes=[raR[e_]])
                        P.op(P.act, lambda e: e.activation(out=IX[:], in_=ib[:], func=AF.Sigmoid, bias=self.par("lru_b_x", d * KC + c)),
                             reads=[bR[4 + gb]], writes=[ixR[e_]])
                        P.op(P.act, lambda e: e.activation(out=RA[:], in_=RA[:], func=AF.Exp, scale=ca[:, d * KC + c:d * KC + c + 1]),
                             reads=[raR[e_], caR], writes=[raR[e_]])
                        P.op(P.act, lambda e: e.activation(out=M[:], in_=RA[:], func=AF.Square), reads=[raR[e_]], writes=[mR[e_]])
                        P.op(P.act, lambda e: e.activation(out=M[:], in_=M[:], func=AF.Sqrt, scale=-1.0, bias=1.0),
                             reads=[mR[e_]], writes=[mR[e_]])
                        P.op(P.dve, lambda e: e.tensor_tensor(out=IX[:], in0=IX[:], in1=X[:], op=ALU.mult),
                             reads=[ixR[e_], xcR[e_]], writes=[ixR[e_]])
                        P.op(P.dve, lambda e: e.tensor_tensor(out=IX[:], in0=IX[:], in1=M[:], op=ALU.mult),
                             reads=[ixR[e_], mR[e_]], writes=[ixR[e_]])
                        if d == 0:
                            init = 0.0 if qi == 0 else hd[:, a - 1:a]
                            P.op(P.dve, lambda e: e.tensor_tensor_scan(out=hd[:, a:a + QT], data0=RA[:], data1=IX[:], initial=init,
                                                                       op0=ALU.mult, op1=ALU.add),
                                 reads=[raR[e_], ixR[e_], hdR], writes=[hdR])
                        else:
                            init = 0.0 if qi == 0 else hd[:, a + QT:a + QT + 1]
                            P.op(P.dve, lambda e: e.tensor_tensor_scan(out=hd[:, a:a + QT][:, ::-1], data0=RA[:, ::-1], data1=IX[:, ::-1],
                                                                       initial=init, op0=ALU.mult, op1=ALU.add),
                                 reads=[raR[e_], ixR[e_], hdR], writes=[hdR])
                self.ws.release()
                P.op(P.dve, lambda e: e.tensor_tensor(out=hs[:], in0=hs[:], in1=hb[:], op=ALU.add), reads=[hsR, hbR], writes=[hsR])
                P.op(P.dve, lambda e: e.tensor_tensor(out=y[:, c, :], in0=hs[:], in1=gg[:], op=ALU.mult), reads=[hsR, ggR], writes=[yR[c]])
            for mp in range(4):
                wv, wR = self.ws.get(("lout", mp))
                wv = wv.rearrange("p (k c) -> p k c", k=KC)
                for mm in range(2):
                    m = mp * 2 + mm
                    for t in range(4):
                        ts = slice(t * 512, (t + 1) * 512)
                        b = pu % 2
                        pu += 1
                        P.mm_group(bank[b][:], [(wv[:, k, mm * 128:(mm + 1) * 128], y[:, k, ts]) for k in range(KC)],
                                   reads=[wR] + yR, writes=[bR[b]])
                        P.op(P.dve, lambda e: e.tensor_tensor(out=self.h[:, m, ts], in0=self.h[:, m, ts], in1=bank[b][:], op=ALU.add),
                             reads=[bR[b], self.hR[m][t]], writes=[self.hR[m][t]])
                self.ws.release()
            P.barrier()

    def dil_layer(self, li):
        P, nc = self.P, self.nc
        DILS = (1, 4, 16)
        with ExitStack() as es:
            cos, sin, tabR = self.rope_tables(es)
            hn = es.enter_context(nc.sbuf_tensor(f"{self.uq()}dl_hn", [128, KC, S], BF16))
            qg = es.enter_context(nc.sbuf_tensor(f"{self.uq()}dl_q", [128, S], BF16))
            kg = es.enter_context(nc.sbuf_tensor(f"{self.uq()}dl_k", [128, S], BF16))
            vpad = es.enter_context(nc.sbuf_tensor(f"{self.uq()}dl_v", [128, 16, 2, 128], BF16))
            onesab = es.enter_context(nc.sbuf_tensor(f"{self.uq()}dl_ones", [128, 2, 128], BF16))
            maskt = es.enter_context(nc.sbuf_tensor(f"{self.uq()}dl_mask", [128, 384], BF16))
            nacc = es.enter_context(nc.sbuf_tensor(f"{self.uq()}dl_nacc", [128, S], F32))
            dacc = es.enter_context(nc.sbuf_tensor(f"{self.uq()}dl_dacc", [128, S], F32))
            oTp = es.enter_context(nc.sbuf_tensor(f"{self.uq()}dl_oT", [128, S], BF16))
            pT = [es.enter_context(nc.sbuf_tensor(f"{self.uq()}dl_pT{i}", [128, 384], BF16)) for i in range(4)]
            qs = [es.enter_context(nc.sbuf_tensor(f"{self.uq()}dl_qs{i}", [128, 512], F32)) for i in range(2)]
            t1 = [es.enter_context(nc.sbuf_tensor(f"{self.uq()}dl_t1{i}", [128, 512], F32)) for i in range(2)]
            qr = [es.enter_context(nc.sbuf_tensor(f"{self.uq()}dl_qr{i}", [128, 512], F32)) for i in range(2)]
            qrR = [Res("qr0"), Res("qr1")]
            bank = [es.enter_context(nc.psum_tensor(f"{self.uq()}dl_ps{i}", [128, 512], F32)) for i in range(8)]
            bR = [Res(f"bank{i}") for i in range(8)]
            hnR, qR, kR, vR, onesR, maskR = Res("hn"), Res("q"), Res("k"), Res("v"), Res("ones"), Res("mask")
            naccR, daccR, oTR = Res("nacc"), Res("dacc"), Res("oT")
            pTR = [Res(f"pT{i}") for i in range(4)]
            qsR = [Res("qs0"), Res("qs1")]
            t1R = [Res("t10"), Res("t11")]

            P.dma(P.pool, maskt[:], self.dram["dil_mask"], writes=[maskR])
            P.op(P.dve, lambda e: e.memset(vpad[:], 0.0), writes=[vR])
            P.op(P.dve, lambda e: e.memset(onesab[:], 0.0), writes=[onesR])
            P.op(P.dve, lambda e: e.memset(onesab[:, 0, 0:64], 1.0), writes=[onesR])
            P.op(P.dve, lambda e: e.memset(onesab[:, 1, 64:128], 1.0), writes=[onesR])
            with ExitStack() as e2:
                rstd = e2.enter_context(nc.sbuf_tensor(f"{self.uq()}dl_rstd", [128, S], F32))
                sq = [e2.enter_context(nc.sbuf_tensor(f"{self.uq()}dl_sq{i}", [128, 512], BF16)) for i in range(2)]
                rstdR = [Res(f"rstd{t}") for t in range(4)]
                sqR = [Res("sq0"), Res("sq1")]
                self.rms_stats(rstd, rstdR, bank[0][:], bR[0], [s_[:] for s_ in sq], sqR)
                for k in range(KC):
                    P.op(P.dve, lambda e: e.scalar_tensor_tensor(
                        out=hn[:, k, :], in0=self.h[:, k, :], scalar=self.par("mix_norm", li * KC + k),
                        in1=rstd[:], op0=ALU.mult, op1=ALU.mult),
                        reads=[self.hR[k][t] for t in range(4)] + rstdR, writes=[hnR])
                P.barrier()

            def chain_tile(ap2d, dil, j):
                if dil == 1:
                    return ap2d[:, j * 512:(j + 1) * 512]
                if dil == 4:
                    return ap2d[:, j:S:4]
                return ap2d.rearrange("p (m r) -> p r m", r=16)[:, 4 * j:4 * j + 4, :]

            def chain_block(ap2d, dil, bi):
                n = S // dil
                nb = n // 128
                r, b = bi // nb, bi % nb
                st = r + dil * b * 128
                return ap2d[:, st:st + dil * 127 + 1:dil]

            pu = ru = pi_ = au = 0
            for c in range(8):
                for g, dil in enumerate(DILS):
                    n = S // dil
                    nb = n // 128
                    for which, dst, dR in ((0, qg, qR), (1, kg, kR)):
                        wv, wR = self.ws.get((("xq", "xk")[which], c, g))
                        wv = wv.rearrange("p (k c) -> p k c", k=KC)
                        for j in range(4):
                            b = pu % 2
                            pu += 1
                            P.mm_group(bank[b][:], [(wv[:, k, :], chain_tile(hn[:, k, :], dil, j)) for k in range(KC)],
                                       reads=[wR, hnR], writes=[bR[b]])
                            r = ru % 2
                            ru += 1
                            cs_, sn_ = chain_tile(cos[:], dil, j), chain_tile(sin[:], dil, j)
                            view = (lambda a_: a_.rearrange("p (r m) -> p r m", r=4)) if dil == 16 else None
                            self.rope_apply(bank[b][:], bR[b], dst[:, j * 512:(j + 1) * 512], dR, cs_, sn_, tabR,
                                            qr[r][:], qrR[r], qs[r][:], qsR[r], t1[r][:], t1R[r], view=view)
                    self.ws.release()
                    wv, wR = self.ws.get(("xv", c, g))
                    wv = wv.rearrange("p (k c) -> p k c", k=KC)
                    for bi4 in range(4):
                        b = pu % 2
                        pu += 1
                        for bb in range(4):
                            bi = bi4 * 4 + bb
                            P.mm_group(bank[b][:, bb * 128:(bb + 1) * 128], [(chain_block(hn[:, k, :], dil, bi), wv[:, k, :]) for k in range(KC)],
                                       reads=[wR, hnR], writes=[bR[b]])
                        pv = bank[b][:].rearrange("p (b c) -> p b c", b=4)
                        P.op(P.act, lambda e: e.activation(out=vpad[:, bi4 * 4:(bi4 + 1) * 4, 0, 0:64], in_=pv[:, :, 0:64], func=AF.Identity),
                             reads=[bR[b]], writes=[vR])
                        P.op(P.act, lambda e: e.activation(out=vpad[:, bi4 * 4:(bi4 + 1) * 4, 1, 64:128], in_=pv[:, :, 64:128], func=AF.Identity),
                             reads=[bR[b]], writes=[vR])
                    self.ws.release()
                    for j in range(4):
                        ab = au % 2
                        au += 1
                        nps, dps = bank[4 + ab], bank[6 + ab]
                        for qq in range(4):
                            qb = j * 4 + qq
                            r, b = qb // nb, qb % nb
                            kbs = [kb for kb in (b - 1, b, b + 1) if 0 <= kb < nb]
                            m0 = 128 * (kbs[0] - (b - 1))
                            nk = len(kbs)
                            pts = []
                            for hh in range(2):
                                hsl = slice(hh * 64, (hh + 1) * 64)
                                sb_ = 2 + hh
                                for jj, kb in enumerate(kbs):
                                    kbi = r * nb + kb
                                    P.op(P.pe, lambda e: e.matmul(bank[sb_][:, jj * 128:(jj + 1) * 128], kg[hsl, kbi * 128:(kbi + 1) * 128],
                                                                  qg[hsl, qb * 128:(qb + 1) * 128], start=True, stop=True),
                                         reads=[kR, qR], writes=[bR[sb_]], signal=(jj == nk - 1))
                                pb = pi_ % 4
                                pi_ += 1
                                pts.append(pb)
                                P.op(P.act, lambda e: e.activation(out=pT[pb][:, 0:nk * 128], in_=bank[sb_][:, 0:nk * 128], func=AF.Exp, scale=0.125),
                                     reads=[bR[sb_]], writes=[pTR[pb]])
                                P.op(P.dve, lambda e: e.tensor_tensor(out=pT[pb][:, 0:nk * 128], in0=pT[pb][:, 0:nk * 128],
                                                                      in1=maskt[:, m0:m0 + nk * 128], op=ALU.mult),
                                     reads=[pTR[pb], maskR], writes=[pTR[pb]])
                            for (acc, accR, lh) in ((nps, bR[4 + ab], None), (dps, bR[6 + ab], onesab)):
                                tot = 2 * nk
                                ii = 0
                                for hh in range(2):
                                    pb = pts[hh]
                                    for jj, kb in enumerate(kbs):
                                        kbi = r * nb + kb
                                        lhsT = vpad[:, kbi, hh, :] if lh is None else onesab[:, hh, :]
                                        P.op(P.pe, lambda e: e.matmul(acc[:, qq * 128:(qq + 1) * 128], lhsT, pT[pb][:, jj * 128:(jj + 1) * 128],
                                                                      start=(ii == 0), stop=(ii == tot - 1)),
                                             reads=[pTR[pb], vR, onesR], writes=[accR], signal=(ii == tot - 1))
                                        ii += 1
                        for (acc, accR, dstt, dstR) in ((nps, bR[4 + ab], nacc, naccR), (dps, bR[6 + ab], dacc, daccR)):
                            if g == 0:
                                P.op(P.act, lambda e: e.activation(out=dstt[:, j * 512:(j + 1) * 512], in_=acc[:], func=AF.Identity),
                                     reads=[accR], writes=[dstR])
                            else:
                                dv = chain_tile(dstt[:], dil, j)
                                av = acc[:].rearrange("p (r m) -> p r m", r=4) if dil == 16 else acc[:]
                                P.op(P.dve, lambda e: e.tensor_tensor(out=dv, in0=dv, in1=av, op=ALU.add),
                                     reads=[accR, dstR], writes=[dstR])
                P.op(P.dve, lambda e: e.reciprocal(out=dacc[:], in_=dacc[:]), reads=[daccR], writes=[daccR])
                P.op(P.dve, lambda e: e.tensor_tensor(out=oTp[:], in0=nacc[:], in1=dacc[:], op=ALU.mult),
                     reads=[naccR, daccR], writes=[oTR])
                wv, wR = self.ws.get(("xo", c))
                for m in range(KC):
                    for t in range(4):
                        ts = slice(t * 512, (t + 1) * 512)
                        b = pu % 2
                        pu += 1
                        P.op(P.pe, lambda e: e.matmul(bank[b][:], wv[:, m * 128:(m + 1) * 128], oTp[:, ts], start=True, stop=True),
                             reads=[wR, oTR], writes=[bR[b]])
                        P.op(P.dve, lambda e: e.tensor_tensor(out=self.h[:, m, ts], in0=self.h[:, m, ts], in1=bank[b][:], op=ALU.add),
                             reads=[bR[b], self.hR[m][t]], writes=[self.hR[m][t]])
                self.ws.release()
            P.barrier()

    def final_store(self, sq_idx, do_norm):
        P, nc = self.P, self.nc
        with ExitStack() as es:
            if do_norm:
                rstd = es.enter_context(nc.sbuf_tensor(f"{self.uq()}fn_rstd", [128, S], F32))
                sq = [es.enter_context(nc.sbuf_tensor(f"{self.uq()}fn_sq{i}", [128, 512], BF16)) for i in range(2)]
                ps_stat = es.enter_context(nc.psum_tensor(f"{self.uq()}fn_psstat", [128, 512], F32))
                rstdR = [Res(f"rstd{t}") for t in range(4)]
                sqR = [Res("sq0"), Res("sq1")]
                psR = Res("ps_stat")
                self.rms_stats(rstd, rstdR, ps_stat[:], psR, [s_[:] for s_ in sq], sqR)
            outd = self.dram["out"][sq_idx].rearrange("(k p) t -> p k t", p=128)
            for k in range(KC):
                if do_norm:
                    P.op(P.dve, lambda e, k=k: e.scalar_tensor_tensor(
                        out=self.h[:, k, :], in0=self.h[:, k, :], scalar=self.par("final_norm", k),
                        in1=rstd[:], op0=ALU.mult, op1=ALU.mult),
                        reads=[self.hR[k][t] for t in range(4)] + rstdR, writes=[self.hR[k][t] for t in range(4)])
                tok = P.dma(P.sp, outd[:, k, :], self.h[:, k, :], reads=[self.hR[k][t] for t in range(4)])
                self.out_toks.append(tok)
            P.barrier()

    def build(self, weight_shapes):
        nc = self.nc
        self.dram = {}
        for name, (shape, dt) in weight_shapes.items():
            self.dram[name] = nc.dram_tensor(name, list(shape), dt, kind="ExternalInput").ap()
        self.dram["out"] = nc.dram_tensor("out", [NSEQ, D, S], F32, kind="ExternalOutput").ap()
        with ExitStack() as es:
            self.P = P = Prog(nc, es)
            self.params = es.enter_context(nc.sbuf_tensor(f"{self.uq()}sb_params", [128, self.npar], F32))
            self.ones_bf = es.enter_context(nc.sbuf_tensor(f"{self.uq()}ones_bf", [128, 128], BF16))
            self.h = es.enter_context(nc.sbuf_tensor(f"{self.uq()}h", [128, KC, S], F32))
            ring_t = es.enter_context(nc.sbuf_tensor(f"{self.uq()}wring", [128, NSLOT, SLOT], BF16))
            self.hR = [[Res(f"h{k}_{t}") for t in range(4)] for k in range(KC)]
            self.ws = WStream(P, ring_t, NSLOT)
            self.out_toks = []
            self.plan_weights()
            parR = Res("params")
            onesR = Res("ones")
            P.dma(P.sp, self.params[:], self.dram["params"], writes=[parR])
            P.op(P.dve, lambda e: e.memset(self.ones_bf[:], 1.0), writes=[onesR])
            self.ident_bf = es.enter_context(nc.sbuf_tensor(f"{self.uq()}ident_bf", [128, 128], BF16))
            P.dma(P.pool, self.ident_bf[:], self.dram["ident"], writes=[Res("ident")])
            P.barrier()
            self.ws.release()
            for sq_idx in range(self.nseq):
                xin = self.dram["xT"][sq_idx].rearrange("(k p) t -> p k t", p=128)
                for k in range(KC):
                    P.dma(P.sp, self.h[:, k, :], xin[:, k, :], writes=[self.hR[k][t] for t in range(4)])
                stopped = False
                for li in range(4):
                    if li in self.mixers:
                        if li == 0:
                            self.pool_layer(li)
                        elif li == 1:
                            self.diff_layer(li)
                        elif li == 2:
                            self.lru_layer(li)
                        else:
                            self.dil_layer(li)
                    if li in self.ffns:
                        self.ffn_layer(li)
                    if self.stop_after is not None and li >= self.stop_after:
                        stopped = True
                        break
                self.final_store(sq_idx, do_norm=not stopped)
            for tok in self.out_toks:
                P._wait(P.sp, *tok)
            P.barrier()
        return nc


def host_consts():
    t = np.arange(S)
    inv = np.zeros((4, S), np.float32)
    for g, win in enumerate((2, 4, 8, 16)):
        lo = np.clip(t - win // 2, 0, S)
        hi = np.clip(t + win - win // 2, 0, S)
        inv[g] = 1.0 / (hi - lo).astype(np.float32)
    edge = np.concatenate([inv[:, 0:8], inv[:, S - 8:S]], axis=1)
    i = np.arange(128)[:, None]
    j = np.arange(128)[None, :]
    mask = np.concatenate([(i - j >= 64), (np.abs(i - j) <= 64), (i - j <= -64)], axis=1).astype(np.float32)
    return {"invcnt": np.ascontiguousarray(np.broadcast_to(edge[None], (128, 4, 16))),
            "ident": np.eye(128, dtype=np.float32),
            "dil_mask": np.ascontiguousarray(mask)}


_CACHE = {}


def kernel(stop_after=None, _mixers=(0, 1, 2, 3), _ffns=(0, 1, 2, 3), **inp):
    inp = {k: np.asarray(v) for k, v in inp.items()}
    pp = pack_params(inp)
    params = pp.build()
    consts = host_consts()
    x = inp["x"]
    xT = np.ascontiguousarray(np.transpose(x, (0, 2, 1)))
    shared = {
        "params": params,
        "invcnt": consts["invcnt"],
        "pool_w": np.ascontiguousarray(inp["pool_w"], dtype=np.float32),
        "ffn_w_up": np.ascontiguousarray(inp["ffn_w_up"], dtype=np.float32),
        "ffn_w_down": np.ascontiguousarray(inp["ffn_w_down"], dtype=np.float32),
        "ident": consts["ident"],
        "dil_mask": consts["dil_mask"],
        "pos_b": np.ascontiguousarray(np.broadcast_to(inp["positions"].astype(np.int32)[None, :], (128, S))),
        "diff_w_qkv_p": permute_qk_cols(inp["diff_w_qkv"][0], [b * 64 for b in range(32)]),
        "diff_w_o": np.ascontiguousarray(inp["diff_w_o"], dtype=np.float32),
        "lru_w_in": np.ascontiguousarray(inp["lru_w_in"], dtype=np.float32),
        "lru_w_out": np.ascontiguousarray(inp["lru_w_out"], dtype=np.float32),
        "lru_bd": lru_blockdiag(inp["lru_w_a"][0], inp["lru_w_x"][0]),
        "dil_w_qkv_p": permute_qk_cols(inp["dil_w_qkv"][0], [((g * 3 + t) * 16 + hd) * 64 for g in range(3) for t in range(2) for hd in range(16)]),
        "dil_w_o": np.ascontiguousarray(inp["dil_w_o"], dtype=np.float32),
    }
    shapes = {k: (v.shape, I32 if v.dtype == np.int32 else F32) for k, v in shared.items()}
    shapes["xT"] = ((NSEQ, D, S), F32)
    key = (stop_after, params.shape[1])
    b = Builder(pp.cols, params.shape[1], stop_after=stop_after, mixers=_mixers, ffns=_ffns)
    import os as _os
    if _os.environ.get("K_DBG_STAGE"):
        b.dbg_stage = int(_os.environ["K_DBG_STAGE"])
    nc = b.build(shapes)
    in_maps = []
    for c in range(NCORES):
        m = dict(shared)
        m["xT"] = xT[c * NSEQ:(c + 1) * NSEQ]
        in_maps.append(m)
    res = run_bass_kernel_spmd(nc, in_maps, core_ids=list(range(NCORES)))
    outT = np.concatenate([r["out"] for r in res.results], axis=0)
    return np.ascontiguousarray(np.transpose(outT, (0, 2, 1))).astype(np.float32)
```

```python
import math
from contextlib import ExitStack

import numpy as np
import concourse.bass as bass
import concourse.mybir as mybir
from concourse.bass_utils import run_bass_kernel_spmd

F32 = mybir.dt.float32
BF16 = mybir.dt.bfloat16
I32 = mybir.dt.int32
AF = mybir.ActivationFunctionType
ALU = mybir.AluOpType
AX = mybir.AxisListType

NCORES = 8
S = 2048
D = 1024
KC = 8
DFF = 2816
FC = 22
NSEQ = 2
EPS = 1e-6
SLOT = 2816
NSLOT = 5


class Res:
    __slots__ = ("name", "writer", "readers")

    def __init__(self, name):
        self.name = name
        self.writer = None
        self.readers = {}


class EngW:
    def __init__(self, name, eng, sem):
        self.name = name
        self.eng = eng
        self.sem = sem
        self.cnt = 0
        self.known = {}
        self.ring = []
        self.ri = 0


class DSem:
    def __init__(self, sem):
        self.sem = sem
        self.cnt = 0


class Prog:
    def __init__(self, nc, es):
        self.nc = nc
        self.es = es
        mk = lambda n, e: EngW(n, e, es.enter_context(nc.semaphore("sem_" + n)))
        self.pe = mk("pe", nc.tensor)
        self.act = mk("act", nc.scalar)
        self.dve = mk("dve", nc.vector)
        self.pool = mk("pool", nc.gpsimd)
        self.sp = mk("sp", nc.sync)
        self.engs = [self.pe, self.act, self.dve, self.pool, self.sp]
        self.dsems = []
        for q, n in ((self.pool, 8), (self.sp, 6)):
            for i in range(n):
                ds = DSem(es.enter_context(nc.semaphore(f"dq_{q.name}{i}")))
                q.ring.append(ds)
                self.dsems.append(ds)

    def _wait(self, E, owner, val):
        if val <= 0:
            return
        k = id(owner)
        if E.known.get(k, 0) >= val:
            return
        if owner is E:
            assert val <= E.cnt, f"self-wait deadlock on {E.name}"
        E.eng.wait_ge(owner.sem, val)
        E.known[k] = val

    def _deps(self, E, reads, writes, skip_same):
        for r in reads:
            if r.writer is not None:
                self._wait(E, *r.writer)
        for w in writes:
            if w.writer is not None and not (skip_same and w.writer[0] is E and E is self.pe):
                self._wait(E, *w.writer)
            for (o, v) in w.readers.values():
                if not (skip_same and o is E):
                    self._wait(E, o, v)

    def _record(self, tok, reads, writes):
        for r in reads:
            k = id(tok[0])
            if k not in r.readers or r.readers[k][1] < tok[1]:
                r.readers[k] = tok
        for w in writes:
            w.writer = tok
            w.readers = {}

    def op(self, E, fn, reads=(), writes=(), signal=True):
        self._deps(E, reads, writes, True)
        ins = fn(E.eng)
        if signal:
            E.cnt += 1
            ins.then_inc(E.sem, 1)
            tok = (E, E.cnt)
        else:
            tok = (E, E.cnt + 1)
        self._record(tok, reads, writes)
        return tok

    def dma(self, Q, out, in_, reads=(), writes=()):
        ds = Q.ring[Q.ri % len(Q.ring)]
        Q.ri += 1
        self._wait(Q, ds, ds.cnt * 16)
        self._deps(Q, reads, writes, False)
        ins = Q.eng.dma_start(out=out, in_=in_)
        ds.cnt += 1
        ins.then_inc(ds.sem, 16)
        tok = (ds, ds.cnt * 16)
        self._record(tok, reads, writes)
        return tok

    def barrier(self):
        for E in self.engs:
            for O in self.engs:
                if O is not E:
                    self._wait(E, O, O.cnt)
            for ds in self.dsems:
                self._wait(E, ds, ds.cnt * 16)

    def mm_group(self, out, pairs, reads, writes):
        n = len(pairs)
        tok = None
        for i, (l, r) in enumerate(pairs):
            tok = self.op(self.pe,
                          lambda e, l=l, r=r, i=i: e.matmul(out, l, r, start=(i == 0), stop=(i == n - 1)),
                          reads=reads, writes=writes, signal=(i == n - 1))
        return tok


class WStream:
    def __init__(self, P, ring_t, nslot):
        self.P = P
        self.ring_t = ring_t
        self.nslot = nslot
        self.res = [Res(f"wslot{i}") for i in range(nslot)]
        self.items = []
        self.issued = 0
        self.consumed = 0

    def plan(self, tag, src, nelem):
        assert nelem <= SLOT
        self.items.append((tag, src, nelem))

    def _issue_upto(self, n):
        n = min(n, len(self.items))
        while self.issued < n:
            i = self.issued
            tag, src, nelem = self.items[i]
            s = i % self.nslot
            dst = self.ring_t[:, s, 0:nelem]
            shp = src.shape
            if len(shp) == 3:
                dst = dst.rearrange("p (a b) -> p a b", a=shp[1])
            elif len(shp) == 4:
                dst = dst.rearrange("p (a b c) -> p a b c", a=shp[1], b=shp[2])
            self.P.dma(self.P.pool, dst, src, writes=[self.res[s]])
            self.issued += 1

    def get(self, tag):
        i = self.consumed
        t, src, nelem = self.items[i]
        assert t == tag, (t, tag)
        self._issue_upto(i + 1)
        s = i % self.nslot
        self.consumed += 1
        view = self.ring_t[:, s, 0:nelem]
        return view, self.res[s]

    def release(self):
        self._issue_upto(self.consumed + self.nslot - 1)


class ParamPack:
    def __init__(self):
        self.cols = {}
        self.n = 0
        self.parts = []

    def add(self, name, arr2d):
        arr2d = np.ascontiguousarray(arr2d, dtype=np.float32)
        assert arr2d.shape[0] == 128
        self.cols[name] = (self.n, arr2d.shape[1])
        self.n += arr2d.shape[1]
        self.parts.append(arr2d)

    def build(self):
        return np.ascontiguousarray(np.concatenate(self.parts, axis=1))


def chan_layout(v, nk):
    v = np.asarray(v, dtype=np.float32)
    lead = v.shape[:-1]
    v = v.reshape(-1, nk, 128)
    return np.transpose(v, (2, 0, 1)).reshape(128, -1)


def pack_params(inp):
    pp = ParamPack()
    pp.add("mix_norm", chan_layout(inp["mix_norm"], KC))
    pp.add("ffn_norm", chan_layout(inp["ffn_norm"], KC))
    pp.add("final_norm", chan_layout(inp["final_norm"], KC))
    pp.add("pool_scale", chan_layout(inp["pool_scale"][0], KC))
    pp.add("ffn_conv_w", chan_layout(inp["ffn_conv_w"], FC))
    pp.add("ffn_conv_b", chan_layout(inp["ffn_conv_b"], FC))
    p = np.arange(128)
    e_ = p % 64
    quad, j = e_ // 32, e_ % 32
    fi = (j % 16) + 16 * quad
    inv = (np.float32(10000.0) ** (-(np.arange(0, 64, 2, dtype=np.float32)) / np.float32(64))).astype(np.float32)
    pp.add("inv_freq", inv[fi][:, None])
    pp.add("rope_sgn", np.where(j < 16, -1.0, 1.0).astype(np.float32)[:, None])
    bc = lambda v: np.broadcast_to(np.asarray(v, np.float32).reshape(1, -1), (128, np.asarray(v).size))
    pp.add("lam_q1", bc(inp["diff_lam_q1"][0]))
    pp.add("lam_k1", bc(inp["diff_lam_k1"][0]))
    pp.add("lam_q2", bc(inp["diff_lam_q2"][0]))
    pp.add("lam_k2", bc(inp["diff_lam_k2"][0]))
    pp.add("subln", bc(inp["diff_subln"][0]))
    pp.add("lru_conv_w", chan_layout(inp["lru_conv_w"][0], KC))
    pp.add("lru_conv_b", chan_layout(inp["lru_conv_b"][0], KC))
    pp.add("lru_b_a", chan_layout(inp["lru_b_a"][0], KC))
    pp.add("lru_b_x", chan_layout(inp["lru_b_x"][0], KC))
    pp.add("lru_lambda", chan_layout(inp["lru_lambda"][0], KC))
    return pp


def rope_perm64():
    e_ = np.arange(64)
    quad, j = e_ // 32, e_ % 32
    fi = (j % 16) + 16 * quad
    return np.where(j < 16, fi, 32 + fi)


def permute_qk_cols(w, qk_blocks):
    w = np.array(w, dtype=np.float32, copy=True)
    perm = rope_perm64()
    for b0 in qk_blocks:
        w[:, b0:b0 + 64] = w[:, b0 + perm]
    return w


def lru_blockdiag(w_a, w_x):
    bd = np.zeros((128, 8, 2, 2, 128), np.float32)
    for d in range(2):
        for ax, w in enumerate((w_a, w_x)):
            for c in range(8):
                for hb in range(2):
                    bd[hb * 64:(hb + 1) * 64, c, d, ax, hb * 64:(hb + 1) * 64] = w[d, 2 * c + hb]
    return bd


class Builder:
    def __init__(self, pp_cols, npar, stop_after=None, mixers=(0, 1, 2, 3), ffns=(0, 1, 2, 3), nseq=NSEQ):
        self.pp_cols = pp_cols
        self.npar = npar
        self.stop_after = stop_after
        self.mixers = set(mixers)
        self.ffns = set(ffns)
        self.nseq = nseq
        self.nc = bass.Bass("TRN2", target_bir_lowering=False)

    def uq(self):
        self._uq = getattr(self, "_uq", 0) + 1
        return f"t{self._uq}_"

    def par(self, name, idx, width=1):
        c0, w = self.pp_cols[name]
        assert idx + width <= w
        return self.params[:, c0 + idx:c0 + idx + width]

    def plan_weights(self):
        ws = self.ws
        for sq in range(self.nseq):
            for li in range(4):
                if self.stop_after is not None and li > self.stop_after:
                    break
                if li == 0 and li in self.mixers:
                    for g in range(4):
                        ws.plan(("pool", g), self.dram["pool_w"][0, g].rearrange("(kc p) e -> p kc e", p=128), 512)
                if li == 1 and li in self.mixers:
                    wq = self.dram["diff_w_qkv_p"].rearrange("(k p) (t hh c) -> p k t hh c", p=128, t=3, hh=8, c=128)
                    wv_ = self.dram["diff_w_qkv_p"].rearrange("(k p) (t gg c) -> p k t gg c", p=128, t=3, gg=4, c=256)
                    wo_ = self.dram["diff_w_o"][0].rearrange("(gg kc p) n -> p gg kc n", p=128, kc=2)
                    for gi in range(4):
                        for which, ti in (("q", 0), ("k", 1)):
                            for hl in range(2):
                                ws.plan(("d" + which, gi, hl), wq[:, :, ti, gi * 2 + hl, :], 1024)
                        ws.plan(("dv", gi), wv_[:, :, 2, gi, :], 2048)
                        ws.plan(("do", gi), wo_[:, gi, :, :], 2048)
                if li == 2 and li in self.mixers:
                    win = self.dram["lru_w_in"][0].rearrange("(k p) (two cc c) -> p two k cc c", p=128, two=2, cc=8, c=128)
                    wout = self.dram["lru_w_out"][0].rearrange("(k p) (mp c) -> p k mp c", p=128, c=256)
                    for c in range(KC):
                        ws.plan(("lin", c), win[:, :, :, c, :], 2048)
                        ws.plan(("lbd", c), self.dram["lru_bd"][:, c], 512)
                    for mp in range(4):
                        ws.plan(("lout", mp), wout[:, :, mp, :], 2048)
                if li == 3 and li in self.mixers:
                    wx = self.dram["dil_w_qkv_p"].rearrange("(k p) (g t hh c) -> p g t k hh c", p=128, g=3, t=3, hh=8, c=128)
                    for c in range(8):
                        for g in range(3):
                            ws.plan(("xq", c, g), wx[:, g, 0, :, c, :], 1024)
                            ws.plan(("xk", c, g), wx[:, g, 1, :, c, :], 1024)
                            ws.plan(("xv", c, g), wx[:, g, 2, :, c, :], 1024)
                        ws.plan(("xo", c), self.dram["dil_w_o"][0][c * 128:(c + 1) * 128, :], 1024)
                if li not in self.ffns:
                    continue
                wup = self.dram["ffn_w_up"][li].rearrange("(k p) (two fc c) -> p two k fc c", p=128, two=2, fc=FC, c=128)
                wdn = self.dram["ffn_w_down"][li].rearrange("(j p) (m c) -> p j m c", p=128, c=128)
                for hf in range(2):
                    for f in range(FC):
                        ws.plan(("up", li, hf, f), wup[:, :, :, f, :], 2048)
                    for m in range(KC):
                        ws.plan(("dn", li, hf, m), wdn[:, :, m, :], 2816)

    def rms_stats(self, rstd, rstdR, ps_bank, psR, sq, sqR):
        P = self.P
        for t in range(4):
            ts = slice(t * 512, (t + 1) * 512)
            for k in range(KC):
                b = k % 2
                P.op(P.act, lambda e, k=k, b=b: e.activation(out=sq[b], in_=self.h[:, k, ts], func=AF.Square),
                     reads=[self.hR[k][t]], writes=[sqR[b]])
                P.op(P.pe, lambda e, k=k, b=b: e.matmul(ps_bank, self.ones_bf[:], sq[b], start=(k == 0), stop=(k == KC - 1)),
                     reads=[sqR[b]], writes=[psR], signal=True)
            P.op(P.dve, lambda e: e.tensor_scalar(out=rstd[:, ts], in0=ps_bank, scalar1=1.0 / D, scalar2=EPS,
                                                  op0=ALU.mult, op1=ALU.add),
                 reads=[psR], writes=[rstdR[t]])
            P.op(P.act, lambda e: e.activation(out=rstd[:, ts], in_=rstd[:, ts], func=AF.Sqrt),
                 reads=[rstdR[t]], writes=[rstdR[t]])
            P.op(P.dve, lambda e: e.reciprocal(out=rstd[:, ts], in_=rstd[:, ts]),
                 reads=[rstdR[t]], writes=[rstdR[t]])

    def pool_layer(self, li):
        P, nc = self.P, self.nc
        PADW = S + 16
        with ExitStack() as es:
            rstd = es.enter_context(nc.sbuf_tensor(f"{self.uq()}pl_rstd", [128, S], F32))
            sq = [es.enter_context(nc.sbuf_tensor(f"{self.uq()}pl_sq{i}", [128, 512], BF16)) for i in range(2)]
            X = es.enter_context(nc.sbuf_tensor(f"{self.uq()}pl_xp", [128, PADW], F32))
            A = es.enter_context(nc.sbuf_tensor(f"{self.uq()}pl_sa", [128, PADW], F32))
            B = es.enter_context(nc.sbuf_tensor(f"{self.uq()}pl_sb", [128, PADW], F32))
            tmpE = es.enter_context(nc.sbuf_tensor(f"{self.uq()}pl_tmpe", [128, 16], F32))
            pooled = es.enter_context(nc.sbuf_tensor(f"{self.uq()}pl_pooled", [128, KC, S], BF16))
            invc = es.enter_context(nc.sbuf_tensor(f"{self.uq()}pl_invc", [128, 4, 16], F32))
            ps_stat = es.enter_context(nc.psum_tensor(f"{self.uq()}pl_ps_stat", [128, 512], F32))
            ps_y = [es.enter_context(nc.psum_tensor(f"{self.uq()}pl_ps_y{i}", [128, 512], F32)) for i in range(2)]
            rstdR = [Res(f"rstd{t}") for t in range(4)]
            sqR = [Res("sq0"), Res("sq1")]
            XR, AR, BR, ER = Res("xp"), Res("sa"), Res("sb"), Res("tmpe")
            pooledR = [Res(f"pooled{k}") for k in range(KC)]
            invcR = Res("invc")
            psR = Res("ps_stat")
            psyR = [Res("psy0"), Res("psy1")]

            P.dma(P.sp, invc[:], self.dram["invcnt"], writes=[invcR])
            self.rms_stats(rstd, rstdR, ps_stat[:], psR, [s_[:] for s_ in sq], sqR)
            P.op(P.dve, lambda e: e.memset(X[:], 0.0), writes=[XR])
            for k in range(KC):
                g = k // 2
                win = (2, 4, 8, 16)[g]
                P.op(P.dve, lambda e: e.scalar_tensor_tensor(
                    out=X[:, 8:8 + S], in0=self.h[:, k, :], scalar=self.par("mix_norm", li * KC + k),
                    in1=rstd[:], op0=ALU.mult, op1=ALU.mult),
                    reads=[self.hR[k][t] for t in range(4)] + rstdR, writes=[XR])
                P.op(P.dve, lambda e: e.tensor_tensor(out=A[:, 1:PADW], in0=X[:, 0:PADW - 1], in1=X[:, 1:PADW], op=ALU.add),
                     reads=[XR], writes=[AR])
                cur, curR, oth, othR = A, AR, B, BR
                lo, hi, sh = 1, PADW, 1
                for lvl in range(g):
                    a0, a1 = lo + sh, hi - sh
                    P.op(P.dve, lambda e: e.tensor_tensor(
                        out=oth[:, a0:a1], in0=cur[:, a0 - sh:a1 - sh], in1=cur[:, a0 + sh:a1 + sh], op=ALU.add),
                        reads=[curR], writes=[othR])
                    lo, hi = a0, a1
                    cur, curR, oth, othR = oth, othR, cur, curR
                    sh *= 2
                assert lo <= 8 and hi >= 8 + S
                P.op(P.dve, lambda e: e.scalar_tensor_tensor(
                    out=pooled[:, k, :], in0=cur[:, 8:8 + S], scalar=1.0 / win, in1=X[:, 8:8 + S],
                    op0=ALU.mult, op1=ALU.subtract),
                    reads=[curR, XR], writes=[pooledR[k]])
                for (c0, e0) in ((0, 0), (S - 8, 8)):
                    P.op(P.dve, lambda e: e.tensor_tensor(
                        out=tmpE[:, e0:e0 + 8], in0=cur[:, 8 + c0:16 + c0], in1=invc[:, g, e0:e0 + 8], op=ALU.mult),
                        reads=[curR, invcR], writes=[ER])
                    P.op(P.dve, lambda e: e.tensor_tensor(
                        out=pooled[:, k, c0:c0 + 8], in0=tmpE[:, e0:e0 + 8], in1=X[:, 8 + c0:16 + c0], op=ALU.subtract),
                        reads=[ER, XR, pooledR[k]], writes=[pooledR[k]])
            u = 0
            for g in range(4):
                wv, wR = self.ws.get(("pool", g))
                wv = wv.rearrange("p (kc e) -> p kc e", kc=2)
                for ec in range(2):
                    m = 2 * g + ec
                    for t in range(4):
                        ts = slice(t * 512, (t + 1) * 512)
                        pb = u % 2
                        u += 1
                        P.mm_group(ps_y[pb][:], [(wv[:, kc, ec * 128:(ec + 1) * 128], pooled[:, 2 * g + kc, ts]) for kc in range(2)],
                                   reads=[wR, pooledR[2 * g], pooledR[2 * g + 1]], writes=[psyR[pb]])
                        P.op(P.dve, lambda e: e.scalar_tensor_tensor(
                            out=self.h[:, m, ts], in0=ps_y[pb][:], scalar=self.par("pool_scale", m),
                            in1=self.h[:, m, ts], op0=ALU.mult, op1=ALU.add),
                            reads=[psyR[pb], self.hR[m][t]], writes=[self.hR[m][t]])
                self.ws.release()
            P.barrier()

    def ffn_layer(self, li):
        P, nc = self.P, self.nc
        HT = 1024
        with ExitStack() as es:
            rstd = es.enter_context(nc.sbuf_tensor(f"{self.uq()}ff_rstd", [128, S], F32))
            sq = [es.enter_context(nc.sbuf_tensor(f"{self.uq()}ff_sq{i}", [128, 512], BF16)) for i in range(2)]
            hn = es.enter_context(nc.sbuf_tensor(f"{self.uq()}ff_hn", [128, KC, HT + 2], BF16))
            act = es.enter_context(nc.sbuf_tensor(f"{self.uq()}ff_act", [128, FC, HT], BF16))
            hn_halo = es.enter_context(nc.sbuf_tensor(f"{self.uq()}ff_hnhalo", [128, KC, 1], BF16))
            haloR = Res("hn_halo")
            gext = [es.enter_context(nc.sbuf_tensor(f"{self.uq()}ff_gext{i}", [128, HT + 2], F32)) for i in range(2)]
            cbuf = [es.enter_context(nc.sbuf_tensor(f"{self.uq()}ff_c{i}", [128, HT], F32)) for i in range(2)]
            ps_g = [es.enter_context(nc.psum_tensor(f"{self.uq()}ff_psg{i}", [128, 512], F32)) for i in range(2)]
            ps_u = [es.enter_context(nc.psum_tensor(f"{self.uq()}ff_psu{i}", [128, 512], F32)) for i in range(2)]
            ps_o = [es.enter_context(nc.psum_tensor(f"{self.uq()}ff_pso{i}", [128, 512], F32)) for i in range(2)]
            ps_h = es.enter_context(nc.psum_tensor(f"{self.uq()}ff_psh", [128, 512], F32))
            ps_stat = es.enter_context(nc.psum_tensor(f"{self.uq()}ff_psstat", [128, 512], F32))
            rstdR = [Res(f"rstd{t}") for t in range(4)]
            sqR = [Res("sq0"), Res("sq1")]
            hnR = Res("hn")
            actR = [Res(f"act{f}") for f in range(FC)]
            gextR = [Res("gext0"), Res("gext1")]
            cR = [Res("c0"), Res("c1")]
            psgR = [Res("psg0"), Res("psg1")]
            psuR = [Res("psu0"), Res("psu1")]
            psoR = [Res("pso0"), Res("pso1")]
            pshR = Res("psh")
            psR = Res("ps_stat")

            self.rms_stats(rstd, rstdR, ps_stat[:], psR, [s_[:] for s_ in sq], sqR)
            unit = 0
            ou = 0
            for hf in range(2):
                t0 = hf * HT
                if hf == 0:
                    c0, c1, ta, tb = 1, HT + 2, 0, HT + 1
                    P.op(P.dve, lambda e: e.memset(hn[:, :, 0:1], 0.0), writes=[hnR])
                else:
                    c0, c1, ta, tb = 1, HT + 1, t0, S
                    P.op(P.dve, lambda e: e.memset(hn[:, :, HT + 1:HT + 2], 0.0), writes=[hnR])
                    P.op(P.dve, lambda e: e.tensor_copy(out=hn[:, :, 0:1], in_=hn_halo[:]), reads=[haloR], writes=[hnR])
                for k in range(KC):
                    P.op(P.dve, lambda e: e.scalar_tensor_tensor(
                        out=hn[:, k, c0:c1], in0=self.h[:, k, ta:tb], scalar=self.par("ffn_norm", li * KC + k),
                        in1=rstd[:, ta:tb], op0=ALU.mult, op1=ALU.mult),
                        reads=[self.hR[k][t] for t in range(4)] + rstdR, writes=[hnR])
                if hf == 0:
                    P.op(P.dve, lambda e: e.tensor_copy(out=hn_halo[:], in_=hn[:, :, HT:HT + 1]), reads=[hnR], writes=[haloR])
                for f in range(FC):
                    wv, wR = self.ws.get(("up", li, hf, f))
                    wv = wv.rearrange("p (two k c) -> p two k c", two=2, k=KC)
                    fb = f % 2
                    for s in range(2):
                        ub = unit % 2
                        unit += 1
                        cs = slice(1 + s * 512, 1 + (s + 1) * 512)
                        P.mm_group(ps_g[ub][:], [(wv[:, 0, k, :], hn[:, k, cs]) for k in range(KC)],
                                   reads=[wR, hnR], writes=[psgR[ub]])
                        P.mm_group(ps_u[ub][:], [(wv[:, 1, k, :], hn[:, k, cs]) for k in range(KC)],
                                   reads=[wR, hnR], writes=[psuR[ub]])
                        P.op(P.act, lambda e, ub=ub, fb=fb, cs=cs: e.activation(out=gext[fb][:, cs], in_=ps_g[ub][:], func=AF.Identity),
                             reads=[psgR[ub]], writes=[gextR[fb]])
                        if s == 0:
                            self._ub0 = ub
                    P.mm_group(ps_h[:, 0:2], [(wv[:, 0, k, :], hn[:, k, 0:HT + 2:HT + 1]) for k in range(KC)],
                               reads=[wR, hnR], writes=[pshR])
                    self.ws.release()
                    P.op(P.act, lambda e, fb=fb: e.activation(out=gext[fb][:, 0:HT + 2:HT + 1], in_=ps_h[:, 0:2], func=AF.Identity),
                         reads=[pshR], writes=[gextR[fb]])
                    cw = lambda j: self.par("ffn_conv_w", (li * 3 + j) * FC + f)
                    cb = self.par("ffn_conv_b", li * FC + f)
                    G, C = gext[fb], cbuf[fb]
                    P.op(P.dve, lambda e, G=G, C=C, cw=cw, cb=cb: e.tensor_scalar(
                        out=C[:], in0=G[:, 1:HT + 1], scalar1=cw(1), scalar2=cb, op0=ALU.mult, op1=ALU.add),
                        reads=[gextR[fb]], writes=[cR[fb]])
                    P.op(P.dve, lambda e, G=G, C=C, cw=cw: e.scalar_tensor_tensor(
                        out=C[:], in0=G[:, 0:HT], scalar=cw(0), in1=C[:], op0=ALU.mult, op1=ALU.add),
                        reads=[gextR[fb], cR[fb]], writes=[cR[fb]])
                    P.op(P.dve, lambda e, G=G, C=C, cw=cw: e.scalar_tensor_tensor(
                        out=C[:], in0=G[:, 2:HT + 2], scalar=cw(2), in1=C[:], op0=ALU.mult, op1=ALU.add),
                        reads=[gextR[fb], cR[fb]], writes=[cR[fb]])
                    P.op(P.act, lambda e, C=C: e.activation(out=C[:], in_=C[:], func=AF.Gelu),
                         reads=[cR[fb]], writes=[cR[fb]])
                    for s in range(2):
                        ub = (unit - 2 + s) % 2
                        P.op(P.dve, lambda e, C=C, s=s, ub=ub, f=f: e.tensor_tensor(
                            out=act[:, f, s * 512:(s + 1) * 512], in0=C[:, s * 512:(s + 1) * 512], in1=ps_u[ub][:], op=ALU.mult),
                            reads=[cR[fb], psuR[ub]], writes=[actR[f]])
                for m in range(KC):
                    wv, wR = self.ws.get(("dn", li, hf, m))
                    wv = wv.rearrange("p (j c) -> p j c", j=FC)
                    for s in range(2):
                        ob = ou % 2
                        ou += 1
                        t = hf * 2 + s
                        ts = slice(t * 512, (t + 1) * 512)
                        P.mm_group(ps_o[ob][:], [(wv[:, j, :], act[:, j, s * 512:(s + 1) * 512]) for j in range(FC)],
                                   reads=[wR] + actR, writes=[psoR[ob]])
                        P.op(P.dve, lambda e, m=m, ts=ts, ob=ob: e.tensor_tensor(
                            out=self.h[:, m, ts], in0=self.h[:, m, ts], in1=ps_o[ob][:], op=ALU.add),
                            reads=[psoR[ob], self.hR[m][t]], writes=[self.hR[m][t]])
                    self.ws.release()
            P.barrier()

    def rope_tables(self, es):
        P, nc = self.P, self.nc
        cos = es.enter_context(nc.sbuf_tensor(f"{self.uq()}cos", [128, S], F32))
        sin = es.enter_context(nc.sbuf_tensor(f"{self.uq()}sin", [128, S], F32))
        tabR = Res("ropetab")
        TWO_PI = 2.0 * math.pi
        with ExitStack() as e2:
            posi = e2.enter_context(nc.sbuf_tensor(f"{self.uq()}posi", [128, S], I32))
            ang = e2.enter_context(nc.sbuf_tensor(f"{self.uq()}ang", [128, S], F32))
            kf = e2.enter_context(nc.sbuf_tensor(f"{self.uq()}kf", [128, S], F32))
            ki = e2.enter_context(nc.sbuf_tensor(f"{self.uq()}ki", [128, S], I32))
            pR, aR, kfR, kiR = Res("posi"), Res("ang"), Res("kf"), Res("ki")
            P.dma(P.sp, posi[:], self.dram["pos_b"], writes=[pR])
            P.op(P.dve, lambda e: e.tensor_copy(out=ang[:], in_=posi[:]), reads=[pR], writes=[aR])
            P.op(P.dve, lambda e: e.tensor_scalar(out=ang[:], in0=ang[:], scalar1=self.par("inv_freq", 0), scalar2=None,
                                                  op0=ALU.mult), reads=[aR], writes=[aR])
            for tab, shift, use_sgn in ((sin, 0.0, True), (cos, math.pi / 2.0, False)):
                P.op(P.dve, lambda e: e.tensor_scalar(out=ki[:], in0=ang[:], scalar1=shift, scalar2=1.0 / TWO_PI,
                                                      op0=ALU.add, op1=ALU.mult), reads=[aR], writes=[kiR])
                P.op(P.dve, lambda e: e.tensor_copy(out=kf[:], in_=ki[:]), reads=[kiR], writes=[kfR])
                P.op(P.dve, lambda e: e.scalar_tensor_tensor(out=kf[:], in0=kf[:], scalar=-TWO_PI, in1=ang[:],
                                                             op0=ALU.mult, op1=ALU.add), reads=[kfR, aR], writes=[kfR])
                P.op(P.dve, lambda e: e.tensor_scalar(out=kf[:], in0=kf[:], scalar1=shift, scalar2=3.141592,
                                                      op0=ALU.add, op1=ALU.min), reads=[kfR], writes=[kfR])
                P.op(P.dve, lambda e: e.tensor_scalar(out=kf[:], in0=kf[:], scalar1=-3.141592, scalar2=None,
                                                      op0=ALU.max), reads=[kfR], writes=[kfR])
                if use_sgn:
                    P.op(P.act, lambda e: e.activation(out=tab[:], in_=kf[:], func=AF.Sin, scale=self.par("rope_sgn", 0)),
                         reads=[kfR], writes=[tabR])
                else:
                    P.op(P.act, lambda e: e.activation(out=tab[:], in_=kf[:], func=AF.Sin),
                         reads=[kfR], writes=[tabR])
            P.barrier()
        return cos, sin, tabR

    def rope_apply(self, ps, psR, dst, dstR, cs_, sn_, tabR, qr, qrR, qs, qsR, t1, t1R, view=None):
        P = self.P
        mask = list(range(16, 32)) + list(range(0, 16))
        v = (lambda a: a) if view is None else view
        P.op(P.act, lambda e: e.activation(out=qr, in_=ps, func=AF.Identity), reads=[psR], writes=[qrR])
        P.op(P.dve, lambda e: e.stream_shuffle(out=qs, in_=qr, mask=mask), reads=[qrR], writes=[qsR])
        P.op(P.dve, lambda e: e.tensor_tensor(out=v(t1), in0=v(qr), in1=cs_, op=ALU.mult), reads=[qrR, tabR], writes=[t1R])
        P.op(P.dve, lambda e: e.tensor_tensor(out=v(qs), in0=v(qs), in1=sn_, op=ALU.mult), reads=[qsR, tabR], writes=[qsR])
        P.op(P.dve, lambda e: e.tensor_tensor(out=dst, in0=t1, in1=qs, op=ALU.add), reads=[t1R, qsR], writes=[dstR])

    def diff_layer(self, li):
        P, nc = self.P, self.nc
        G = 2
        NG = 8 // G
        lam_init = 0.8 - 0.6 * math.exp(-0.3 * li)
        with ExitStack() as es:
            cos, sin, tabR = self.rope_tables(es)
            hn = es.enter_context(nc.sbuf_tensor(f"{self.uq()}da_hn", [128, KC, S], BF16))
            bank = {i: es.enter_context(nc.psum_tensor(f"{self.uq()}da_ps{i}", [128, 512], F32)) for i in (0, 1, 6, 7)}
            bR = {i: Res(f"bank{i}") for i in (0, 1, 6, 7)}
            rstdR = [Res(f"rstd{t}") for t in range(4)]
            sqR = [Res("sq0"), Res("sq1")]
            hnR = Res("hn")
            with ExitStack() as e2:
                rstd = e2.enter_context(nc.sbuf_tensor(f"{self.uq()}da_rstd", [128, S], F32))
                sq = [e2.enter_context(nc.sbuf_tensor(f"{self.uq()}da_sq{i}", [128, 512], BF16)) for i in range(2)]
                self.rms_stats(rstd, rstdR, bank[7][:], bR[7], [s_[:] for s_ in sq], sqR)
                for k in range(KC):
                    P.op(P.dve, lambda e: e.scalar_tensor_tensor(
                        out=hn[:, k, :], in0=self.h[:, k, :], scalar=self.par("mix_norm", li * KC + k),
                        in1=rstd[:], op0=ALU.mult, op1=ALU.mult),
                        reads=[self.hR[k][t] for t in range(4)] + rstdR, writes=[hnR])
                P.barrier()

            qg = es.enter_context(nc.sbuf_tensor(f"{self.uq()}da_q", [128, G, S], BF16))
            kg = es.enter_context(nc.sbuf_tensor(f"{self.uq()}da_k", [128, G, S], BF16))
            vaug = es.enter_context(nc.sbuf_tensor(f"{self.uq()}da_v", [128, 16, G, 130], BF16))
            oT = es.enter_context(nc.sbuf_tensor(f"{self.uq()}da_oT", [128, G, S], BF16))
            pT = [es.enter_context(nc.sbuf_tensor(f"{self.uq()}da_pT{i}", [128, 512], BF16)) for i in range(3)]
            qs = [es.enter_context(nc.sbuf_tensor(f"{self.uq()}da_qs{i}", [128, 512], F32)) for i in range(2)]
            t1 = [es.enter_context(nc.sbuf_tensor(f"{self.uq()}da_t1{i}", [128, 512], F32)) for i in range(2)]
            qr = [es.enter_context(nc.sbuf_tensor(f"{self.uq()}da_qr{i}", [128, 512], F32)) for i in range(2)]
            qrR = [Res("qr0"), Res("qr1")]
            tmp = es.enter_context(nc.sbuf_tensor(f"{self.uq()}da_tmp", [128, 4, 128], F32))
            tmpR = Res("tmp")
            OA = es.enter_context(nc.psum_tensor(f"{self.uq()}da_oa", [128, 4, 512], F32))
            oaR = Res("oa")
            o1 = es.enter_context(nc.sbuf_tensor(f"{self.uq()}da_o1", [128, 4, 128], F32))
            sm = es.enter_context(nc.sbuf_tensor(f"{self.uq()}da_sm", [128, 4, 8], F32))
            onb = es.enter_context(nc.sbuf_tensor(f"{self.uq()}da_onb", [128, 4, 128], BF16))
            lamt = es.enter_context(nc.sbuf_tensor(f"{self.uq()}da_lam", [128, 8], F32))
            lprod = es.enter_context(nc.sbuf_tensor(f"{self.uq()}da_lprod", [128, 2, 64], F32))
            subl = es.enter_context(nc.sbuf_tensor(f"{self.uq()}da_subl", [128, 128], F32))
            qR = [Res(f"q{i}") for i in range(G)]
            kR = [Res(f"k{i}") for i in range(G)]
            vR = [Res(f"v{i}") for i in range(G)]
            oTR = [Res(f"oT{i}") for i in range(G)]
            pTR = [Res(f"pT{i}") for i in range(3)]
            qsR = [Res("qs0"), Res("qs1")]
            t1R = [Res("t10"), Res("t11")]
            o1R = Res("o1")
            smR = Res("sm")
            onbR = Res("onb")
            pending = []
            lamR = Res("lam")
            sublR = Res("subl")

            P.op(P.dve, lambda e: e.tensor_tensor(out=lprod[:, 0, :], in0=self.par("lam_q1", 0, 64), in1=self.par("lam_k1", 0, 64), op=ALU.mult),
                 writes=[lamR])
            P.op(P.dve, lambda e: e.tensor_tensor(out=lprod[:, 1, :], in0=self.par("lam_q2", 0, 64), in1=self.par("lam_k2", 0, 64), op=ALU.mult),
                 writes=[lamR])
            P.op(P.dve, lambda e: e.reduce_sum(out=lamt[:, 0:1], in_=lprod[:, 0, :], axis=AX.X), reads=[lamR], writes=[lamR])
            P.op(P.dve, lambda e: e.reduce_sum(out=lamt[:, 1:2], in_=lprod[:, 1, :], axis=AX.X), reads=[lamR], writes=[lamR])
            P.op(P.act, lambda e: e.activation(out=lamt[:, 2:4], in_=lamt[:, 0:2], func=AF.Exp), reads=[lamR], writes=[lamR])
            P.op(P.dve, lambda e: e.tensor_tensor(out=lamt[:, 4:5], in0=lamt[:, 3:4], in1=lamt[:, 2:3], op=ALU.subtract),
                 reads=[lamR], writes=[lamR])
            P.op(P.dve, lambda e: e.tensor_scalar(out=lamt[:, 5:6], in0=lamt[:, 4:5], scalar1=-lam_init, scalar2=None, op0=ALU.add),
                 reads=[lamR], writes=[lamR])
            neg_lam = lamt[:, 5:6]
            P.op(P.dve, lambda e: e.tensor_scalar(out=subl[:], in0=self.par("subln", 0, 128), scalar1=1.0 - lam_init, scalar2=None, op0=ALU.mult),
                 writes=[sublR])
            P.op(P.dve, lambda e: e.memset(vaug[:, :, :, 128:130], 1.0), writes=vR)

            stage = getattr(self, "dbg_stage", 99)
            pu = 0
            ru = 0
            si = 0
            tu = 0
            for gi in range(NG):
                for which, dst, dR in (("q", qg, qR), ("k", kg, kR)):
                    for hl in range(G):
                        wv, wR = self.ws.get(("d" + which, gi, hl))
                        wv = wv.rearrange("p (k c) -> p k c", k=KC)
                        for t in range(4):
                            ts = slice(t * 512, (t + 1) * 512)
                            b = pu % 2
                            pu += 1
                            P.mm_group(bank[b][:], [(wv[:, k, :], hn[:, k, ts]) for k in range(KC)],
                                       reads=[wR, hnR], writes=[bR[b]])
                            r = ru % 2
                            ru += 1
                            self.rope_apply(bank[b][:], bR[b], dst[:, hl, ts], dR[hl], cos[:, ts], sin[:, ts], tabR,
                                            qr[r][:], qrR[r], qs[r][:], qsR[r], t1[r][:], t1R[r])
                        self.ws.release()
                wv, wR = self.ws.get(("dv", gi))
                wv = wv.rearrange("p (k c) -> p k c", k=KC)
                for blk in range(16):
                    b = pu % 2
                    pu += 1
                    bs = slice(blk * 128, (blk + 1) * 128)
                    P.mm_group(bank[b][:, 0:256], [(hn[:, k, bs], wv[:, k, :]) for k in range(KC)],
                               reads=[wR, hnR], writes=[bR[b]])
                    P.op(P.act, lambda e: e.activation(out=vaug[:, blk, :, 0:128],
                                                       in_=bank[b][:, 0:256].rearrange("p (g c) -> p g c", g=G), func=AF.Identity),
                         reads=[bR[b]], writes=vR)
                self.ws.release()
                def emit_score(st):
                    nonlocal si
                    hl, qt, c, kb = st
                    sb_ = si % 2
                    pb = si % 3
                    si += 1
                    ps_ = slice(c * 64, (c + 1) * 64)
                    P.op(P.pe, lambda e: e.matmul(bank[sb_][:], kg[ps_, hl, kb * 128:(kb + 1) * 128], qg[ps_, hl, qt * 512:(qt + 1) * 512],
                                                  start=True, stop=True),
                         reads=[kR[hl], qR[hl]], writes=[bR[sb_]])
                    return sb_, pb

                def emit_exp_pv(st, sb_, pb):
                    hl, qt, c, kb = st
                    P.op(P.act, lambda e: e.activation(out=pT[pb][:], in_=bank[sb_][:], func=AF.Exp, scale=0.125),
                         reads=[bR[sb_]], writes=[pTR[pb]])
                    for qb in range(4):
                        P.op(P.pe, lambda e: e.matmul(OA[:, qb, 0:130], pT[pb][:, qb * 128:(qb + 1) * 128],
                                                      vaug[:, kb, hl, :], start=(kb == 0), stop=(kb == 15)),
                             reads=[pTR[pb], vR[hl]], writes=[oaR], signal=(qb == 3))

                def emit_epilogue(hl, qt, c):
                    if c == 0:
                        for fn in pending:
                            fn()
                        pending.clear()
                    bc = lambda ap: ap.broadcast_to([128, 4, 128])
                    if c == 0:
                        P.op(P.dve, lambda e: e.reciprocal(out=sm[:, :, 0:1], in_=OA[:, :, 128:129]), reads=[oaR], writes=[smR])
                        P.op(P.dve, lambda e: e.tensor_tensor(out=o1[:], in0=OA[:, :, 0:128], in1=bc(sm[:, :, 0:1]), op=ALU.mult),
                             reads=[oaR, smR], writes=[o1R])
                    else:
                        P.op(P.dve, lambda e: e.reciprocal(out=sm[:, :, 1:2], in_=OA[:, :, 128:129]), reads=[oaR], writes=[smR])
                        P.op(P.dve, lambda e: e.tensor_scalar(out=sm[:, :, 2:3], in0=sm[:, :, 1:2], scalar1=neg_lam, scalar2=None,
                                                              op0=ALU.mult), reads=[smR, lamR], writes=[smR])
                        P.op(P.dve, lambda e: e.tensor_tensor(out=tmp[:], in0=OA[:, :, 0:128], in1=bc(sm[:, :, 2:3]), op=ALU.mult),
                             reads=[oaR, smR], writes=[tmpR])
                        P.op(P.dve, lambda e: e.tensor_tensor(out=o1[:], in0=o1[:], in1=tmp[:], op=ALU.add),
                             reads=[o1R, tmpR], writes=[o1R])
                        P.op(P.dve, lambda e: e.tensor_tensor(out=tmp[:], in0=o1[:], in1=o1[:], op=ALU.mult),
                             reads=[o1R], writes=[tmpR])
                        P.op(P.dve, lambda e: e.reduce_sum(out=sm[:, :, 3], in_=tmp[:], axis=AX.X), reads=[tmpR], writes=[smR])
                        P.op(P.dve, lambda e: e.tensor_scalar(out=sm[:, :, 4:5], in0=sm[:, :, 3:4], scalar1=1.0 / 128.0, scalar2=1e-5,
                                                              op0=ALU.mult, op1=ALU.add), reads=[smR], writes=[smR])
                        P.op(P.act, lambda e: e.activation(out=sm[:, :, 5:6], in_=sm[:, :, 4:5], func=AF.Sqrt), reads=[smR], writes=[smR])
                        P.op(P.dve, lambda e: e.reciprocal(out=sm[:, :, 6:7], in_=sm[:, :, 5:6]), reads=[smR], writes=[smR])
                        P.op(P.dve, lambda e: e.tensor_tensor(out=tmp[:], in0=o1[:], in1=bc(sm[:, :, 6:7]), op=ALU.mult),
                             reads=[o1R, smR], writes=[tmpR])
                        P.op(P.dve, lambda e: e.tensor_tensor(out=onb[:], in0=tmp[:], in1=bc(subl[:].unsqueeze(1)), op=ALU.mult),
                             reads=[tmpR, sublR], writes=[onbR])
                        for qb in range(4):
                            def tr(qb=qb, hl=hl, q0=qt * 512 + qb * 128):
                                nonlocal tu
                                ib = tu % 2
                                tu += 1
                                tps = bank[6 + ib][:].bitcast(BF16)
                                P.op(P.pe, lambda e: e.transpose(tps[:, 0:128], onb[:, qb, :], self.ident_bf[:]),
                                     reads=[onbR], writes=[bR[6 + ib]])
                                P.op(P.act, lambda e: e.activation(out=oT[:, hl, q0:q0 + 128], in_=tps[:, 0:128], func=AF.Identity),
                                     reads=[bR[6 + ib]], writes=[oTR[hl]])
                            pending.append(tr)

                steps = [(hl, qt, c, kb) for hl in range(G) for qt in range(4) for c in range(2) for kb in range(16)] if stage >= 3 else []
                nxt = emit_score(steps[0]) if steps else None
                for i, st in enumerate(steps):
                    cur = nxt
                    if i + 1 < len(steps):
                        nxt = emit_score(steps[i + 1])
                    emit_exp_pv(st, *cur)
                    if st[3] == 15:
                        emit_epilogue(st[0], st[1], st[2])
                for fn in pending:
                    fn()
                pending.clear()
                if stage < 3:
                    P.op(P.dve, lambda e: e.memset(oT[:], 0.0), writes=oTR)
                wv, wR = self.ws.get(("do", gi))
                wv = wv.rearrange("p (kc c) -> p kc c", kc=G)
                for m in range(KC):
                    for t in range(4):
                        ts = slice(t * 512, (t + 1) * 512)
                        b = pu % 2
                        pu += 1
                        P.mm_group(bank[b][:], [(wv[:, kc, m * 128:(m + 1) * 128], oT[:, kc, ts]) for kc in range(G)],
                                   reads=[wR] + oTR, writes=[bR[b]])
                        P.op(P.dve, lambda e: e.tensor_tensor(out=self.h[:, m, ts], in0=self.h[:, m, ts], in1=bank[b][:], op=ALU.add),
                             reads=[bR[b], self.hR[m][t]], writes=[self.hR[m][t]])
                self.ws.release()
            P.barrier()

    def lru_layer(self, li):
        P, nc = self.P, self.nc
        QT = 512
        with ExitStack() as es:
            hn = es.enter_context(nc.sbuf_tensor(f"{self.uq()}lr_hn", [128, KC, S], BF16))
            y = es.enter_context(nc.sbuf_tensor(f"{self.uq()}lr_y", [128, KC, S], BF16))
            bank = [es.enter_context(nc.psum_tensor(f"{self.uq()}lr_ps{i}", [128, 512], F32)) for i in range(8)]
            bR = [Res(f"bank{i}") for i in range(8)]
            hnR = Res("hn")
            yR = [Res(f"y{k}") for k in range(KC)]
            with ExitStack() as e2:
                rstd = e2.enter_context(nc.sbuf_tensor(f"{self.uq()}lr_rstd", [128, S], F32))
                sq = [e2.enter_context(nc.sbuf_tensor(f"{self.uq()}lr_sq{i}", [128, 512], BF16)) for i in range(2)]
                rstdR = [Res(f"rstd{t}") for t in range(4)]
                sqR = [Res("sq0"), Res("sq1")]
                self.rms_stats(rstd, rstdR, bank[7][:], bR[7], [s_[:] for s_ in sq], sqR)
                for k in range(KC):
                    P.op(P.dve, lambda e: e.scalar_tensor_tensor(
                        out=hn[:, k, :], in0=self.h[:, k, :], scalar=self.par("mix_norm", li * KC + k),
                        in1=rstd[:], op0=ALU.mult, op1=ALU.mult),
                        reads=[self.hR[k][t] for t in range(4)] + rstdR, writes=[hnR])
                P.barrier()
            usb = es.enter_context(nc.sbuf_tensor(f"{self.uq()}lr_u", [128, S + 6], F32))
            gg = es.enter_context(nc.sbuf_tensor(f"{self.uq()}lr_gg", [128, S], BF16))
            hs = es.enter_context(nc.sbuf_tensor(f"{self.uq()}lr_hs", [128, S], F32))
            hb = es.enter_context(nc.sbuf_tensor(f"{self.uq()}lr_hb", [128, S], F32))
            xc = [es.enter_context(nc.sbuf_tensor(f"{self.uq()}lr_xc{i}", [128, QT], F32)) for i in range(2)]
            xcb = [es.enter_context(nc.sbuf_tensor(f"{self.uq()}lr_xcb{i}", [128, QT], BF16)) for i in range(2)]
            ra = [es.enter_context(nc.sbuf_tensor(f"{self.uq()}lr_ra{i}", [128, QT], F32)) for i in range(2)]
            ix = [es.enter_context(nc.sbuf_tensor(f"{self.uq()}lr_ix{i}", [128, QT], F32)) for i in range(2)]
            mm_ = [es.enter_context(nc.sbuf_tensor(f"{self.uq()}lr_m{i}", [128, QT], F32)) for i in range(2)]
            ca = es.enter_context(nc.sbuf_tensor(f"{self.uq()}lr_ca", [128, 16], F32))
            usbR, ggR, hsR, hbR = Res("usb"), Res("gg"), Res("hs"), Res("hb")
            xcR = [Res("xc0"), Res("xc1")]
            xcbR = [Res("xcb0"), Res("xcb1")]
            raR = [Res("ra0"), Res("ra1")]
            ixR = [Res("ix0"), Res("ix1")]
            mR = [Res("m0"), Res("m1")]
            caR = Res("ca")

            P.op(P.act, lambda e: e.activation(out=ca[:], in_=self.par("lru_lambda", 0, 16), func=AF.Exp, scale=-1.0), writes=[caR])
            P.op(P.act, lambda e: e.activation(out=ca[:], in_=ca[:], func=AF.Ln, bias=1.0), reads=[caR], writes=[caR])
            P.op(P.dve, lambda e: e.tensor_scalar(out=ca[:], in0=ca[:], scalar1=-8.0, scalar2=None, op0=ALU.mult), reads=[caR], writes=[caR])
            P.op(P.dve, lambda e: e.memset(usb[:], 0.0), writes=[usbR])

            pu = 0
            gu = 0
            eu = 0
            for c in range(KC):
                wv, wR = self.ws.get(("lin", c))
                wv = wv.rearrange("p (two k c) -> p two k c", k=KC, two=2)
                bdv, bdR = self.ws.get(("lbd", c))
                bdv = bdv.rearrange("p (d ax c) -> p d ax c", d=2, ax=2)
                for t in range(4):
                    ts = slice(t * 512, (t + 1) * 512)
                    b = pu % 2
                    pu += 1
                    P.mm_group(bank[b][:], [(wv[:, 0, k, :], hn[:, k, ts]) for k in range(KC)], reads=[wR, hnR], writes=[bR[b]])
                    P.op(P.act, lambda e: e.activation(out=gg[:, ts], in_=bank[b][:], func=AF.Gelu_apprx_tanh),
                         reads=[bR[b]], writes=[ggR])
                    b = pu % 2
                    pu += 1
                    P.mm_group(bank[b][:], [(wv[:, 1, k, :], hn[:, k, ts]) for k in range(KC)], reads=[wR, hnR], writes=[bR[b]])
                    P.op(P.act, lambda e: e.activation(out=usb[:, 3 + t * 512:3 + (t + 1) * 512], in_=bank[b][:], func=AF.Identity),
                         reads=[bR[b]], writes=[usbR])
                units = [(d, qi, q) for d in range(2) for qi, q in enumerate((0, 1, 2, 3) if d == 0 else (3, 2, 1, 0))]

                def stage1(u):
                    nonlocal eu, gu
                    d, qi, q = u
                    a = q * QT
                    e_ = eu % 2
                    eu += 1
                    X, XB, RA, IX, M = xc[e_], xcb[e_], ra[e_], ix[e_], mm_[e_]
                    cw = lambda j: self.par("lru_conv_w", (d * 4 + j) * KC + c)
                    cb = self.par("lru_conv_b", d * KC + c)
                    if d == 0:
                        off = lambda j: a + j
                    else:
                        off = lambda j: a + 6 - j
                    P.op(P.dve, lambda e: e.tensor_scalar(out=X[:], in0=usb[:, off(0):off(0) + QT], scalar1=cw(0), scalar2=cb,
                                                          op0=ALU.mult, op1=ALU.add), reads=[usbR], writes=[xcR[e_]])
                    for j in range(1, 4):
                        P.op(P.dve, lambda e: e.scalar_tensor_tensor(out=X[:], in0=usb[:, off(j):off(j) + QT], scalar=cw(j), in1=X[:],
                                                                     op0=ALU.mult, op1=ALU.add),
                             reads=[usbR, xcR[e_]], writes=[xcR[e_]])
                    P.op(P.act, lambda e: e.activation(out=XB[:], in_=X[:], func=AF.Identity), reads=[xcR[e_]], writes=[xcbR[e_]])
                    gb = gu % 2
                    gu += 1
                    rb, ib = bank[2 + gb], bank[4 + gb]
                    P.op(P.pe, lambda e: e.matmul(rb[:], bdv[:, d, 0, :], XB[:], start=True, stop=True),
                         reads=[bdR, xcbR[e_]], writes=[bR[2 + gb]])
                    P.op(P.pe, lambda e: e.matmul(ib[:], bdv[:, d, 1, :], XB[:], start=True, stop=True),
                         reads=[bdR, xcbR[e_]], writes=[bR[4 + gb]])
                    P.op(P.act, lambda e: e.activation(out=RA[:], in_=rb[:], func=AF.Sigmoid, bias=self.par("lru_b_a", d * KC + c)),
                         reads=[bR[2 + gb]], writes=[raR[e_]])
                    P.op(P.act, lambda e: e.activation(out=IX[:], in_=ib[:], func=AF.Sigmoid, bias=self.par("lru_b_x", d * KC + c)),
                         reads=[bR[4 + gb]], writes=[ixR[e_]])
                    P.op(P.act, lambda e: e.activation(out=RA[:], in_=RA[:], func=AF.Exp, scale=ca[:, d * KC + c:d * KC + c + 1]),
                         reads=[raR[e_], caR], writes=[raR[e_]])
                    P.op(P.act, lambda e: e.activation(out=M[:], in_=RA[:], func=AF.Square), reads=[raR[e_]], writes=[mR[e_]])
                    P.op(P.act, lambda e: e.activation(out=M[:], in_=M[:], func=AF.Sqrt, scale=-1.0, bias=1.0),
                         reads=[mR[e_]], writes=[mR[e_]])
                    return e_

                def stage2(u, e_):
                    d, qi, q = u
                    a = q * QT
                    X, XB, RA, IX, M = xc[e_], xcb[e_], ra[e_], ix[e_], mm_[e_]
                    hd, hdR = (hs, hsR) if d == 0 else (hb, hbR)
                    P.op(P.dve, lambda e: e.tensor_tensor(out=IX[:], in0=IX[:], in1=X[:], op=ALU.mult),
                         reads=[ixR[e_], xcR[e_]], writes=[ixR[e_]])
                    P.op(P.dve, lambda e: e.tensor_tensor(out=IX[:], in0=IX[:], in1=M[:], op=ALU.mult),
                         reads=[ixR[e_], mR[e_]], writes=[ixR[e_]])
                    if d == 0:
                        init = 0.0 if qi == 0 else hd[:, a - 1:a]
                        P.op(P.dve, lambda e: e.tensor_tensor_scan(out=hd[:, a:a + QT], data0=RA[:], data1=IX[:], initial=init,
                                                                   op0=ALU.mult, op1=ALU.add),
                             reads=[raR[e_], ixR[e_], hdR], writes=[hdR])
                    else:
                        init = 0.0 if qi == 0 else hd[:, a + QT:a + QT + 1]
                        P.op(P.dve, lambda e: e.tensor_tensor_scan(out=hd[:, a:a + QT][:, ::-1], data0=RA[:, ::-1], data1=IX[:, ::-1],
                                                                   initial=init, op0=ALU.mult, op1=ALU.add),
                             reads=[raR[e_], ixR[e_], hdR], writes=[hdR])

                nxt = stage1(units[0])
                for ui, u in enumerate(units):
                    cur = nxt
                    if ui + 1 < len(units):
                        nxt = stage1(units[ui + 1])
                    stage2(u, cur)
                self.ws.release()
                P.op(P.dve, lambda e: e.tensor_tensor(out=hs[:], in0=hs[:], in1=hb[:], op=ALU.add), reads=[hsR, hbR], writes=[hsR])
                P.op(P.dve, lambda e: e.tensor_tensor(out=y[:, c, :], in0=hs[:], in1=gg[:], op=ALU.mult), reads=[hsR, ggR], writes=[yR[c]])
            for mp in range(4):
                wv, wR = self.ws.get(("lout", mp))
                wv = wv.rearrange("p (k c) -> p k c", k=KC)
                for mm in range(2):
                    m = mp * 2 + mm
                    for t in range(4):
                        ts = slice(t * 512, (t + 1) * 512)
                        b = pu % 2
                        pu += 1
                        P.mm_group(bank[b][:], [(wv[:, k, mm * 128:(mm + 1) * 128], y[:, k, ts]) for k in range(KC)],
                                   reads=[wR] + yR, writes=[bR[b]])
                        P.op(P.dve, lambda e: e.tensor_tensor(out=self.h[:, m, ts], in0=self.h[:, m, ts], in1=bank[b][:], op=ALU.add),
                             reads=[bR[b], self.hR[m][t]], writes=[self.hR[m][t]])
                self.ws.release()
            P.barrier()

    def dil_layer(self, li):
        P, nc = self.P, self.nc
        DILS = (1, 4, 16)
        with ExitStack() as es:
            cos, sin, tabR = self.rope_tables(es)
            hn = es.enter_context(nc.sbuf_tensor(f"{self.uq()}dl_hn", [128, KC, S], BF16))
            qg = es.enter_context(nc.sbuf_tensor(f"{self.uq()}dl_q", [128, S], BF16))
            kg = es.enter_context(nc.sbuf_tensor(f"{self.uq()}dl_k", [128, S], BF16))
            vpad = es.enter_context(nc.sbuf_tensor(f"{self.uq()}dl_v", [128, 16, 2, 128], BF16))
            onesab = es.enter_context(nc.sbuf_tensor(f"{self.uq()}dl_ones", [128, 2, 128], BF16))
            maskt = es.enter_context(nc.sbuf_tensor(f"{self.uq()}dl_mask", [128, 384], BF16))
            nacc = es.enter_context(nc.sbuf_tensor(f"{self.uq()}dl_nacc", [128, S], F32))
            dacc = es.enter_context(nc.sbuf_tensor(f"{self.uq()}dl_dacc", [128, S], F32))
            oTp = es.enter_context(nc.sbuf_tensor(f"{self.uq()}dl_oT", [128, S], BF16))
            pT = [es.enter_context(nc.sbuf_tensor(f"{self.uq()}dl_pT{i}", [128, 384], BF16)) for i in range(4)]
            qs = [es.enter_context(nc.sbuf_tensor(f"{self.uq()}dl_qs{i}", [128, 512], F32)) for i in range(2)]
            t1 = [es.enter_context(nc.sbuf_tensor(f"{self.uq()}dl_t1{i}", [128, 512], F32)) for i in range(2)]
            qr = [es.enter_context(nc.sbuf_tensor(f"{self.uq()}dl_qr{i}", [128, 512], F32)) for i in range(2)]
            qrR = [Res("qr0"), Res("qr1")]
            bank = [es.enter_context(nc.psum_tensor(f"{self.uq()}dl_ps{i}", [128, 512], F32)) for i in range(8)]
            bR = [Res(f"bank{i}") for i in range(8)]
            hnR, qR, kR, vR, onesR, maskR = Res("hn"), Res("q"), Res("k"), Res("v"), Res("ones"), Res("mask")
            naccR, daccR, oTR = Res("nacc"), Res("dacc"), Res("oT")
            pTR = [Res(f"pT{i}") for i in range(4)]
            qsR = [Res("qs0"), Res("qs1")]
            t1R = [Res("t10"), Res("t11")]

            P.dma(P.pool, maskt[:], self.dram["dil_mask"], writes=[maskR])
            P.op(P.dve, lambda e: e.memset(vpad[:], 0.0), writes=[vR])
            P.op(P.dve, lambda e: e.memset(onesab[:], 0.0), writes=[onesR])
            P.op(P.dve, lambda e: e.memset(onesab[:, 0, 0:64], 1.0), writes=[onesR])
            P.op(P.dve, lambda e: e.memset(onesab[:, 1, 64:128], 1.0), writes=[onesR])
            with ExitStack() as e2:
                rstd = e2.enter_context(nc.sbuf_tensor(f"{self.uq()}dl_rstd", [128, S], F32))
                sq = [e2.enter_context(nc.sbuf_tensor(f"{self.uq()}dl_sq{i}", [128, 512], BF16)) for i in range(2)]
                rstdR = [Res(f"rstd{t}") for t in range(4)]
                sqR = [Res("sq0"), Res("sq1")]
                self.rms_stats(rstd, rstdR, bank[0][:], bR[0], [s_[:] for s_ in sq], sqR)
                for k in range(KC):
                    P.op(P.dve, lambda e: e.scalar_tensor_tensor(
                        out=hn[:, k, :], in0=self.h[:, k, :], scalar=self.par("mix_norm", li * KC + k),
                        in1=rstd[:], op0=ALU.mult, op1=ALU.mult),
                        reads=[self.hR[k][t] for t in range(4)] + rstdR, writes=[hnR])
                P.barrier()

            def chain_tile(ap2d, dil, j):
                if dil == 1:
                    return ap2d[:, j * 512:(j + 1) * 512]
                if dil == 4:
                    return ap2d[:, j:S:4]
                return ap2d.rearrange("p (m r) -> p r m", r=16)[:, 4 * j:4 * j + 4, :]

            def chain_block(ap2d, dil, bi):
                n = S // dil
                nb = n // 128
                r, b = bi // nb, bi % nb
                st = r + dil * b * 128
                return ap2d[:, st:st + dil * 127 + 1:dil]

            pu = ru = pi_ = au = 0
            for c in range(8):
                for g, dil in enumerate(DILS):
                    n = S // dil
                    nb = n // 128
                    for which, dst, dR in ((0, qg, qR), (1, kg, kR)):
                        wv, wR = self.ws.get((("xq", "xk")[which], c, g))
                        wv = wv.rearrange("p (k c) -> p k c", k=KC)
                        for j in range(4):
                            b = pu % 2
                            pu += 1
                            P.mm_group(bank[b][:], [(wv[:, k, :], chain_tile(hn[:, k, :], dil, j)) for k in range(KC)],
                                       reads=[wR, hnR], writes=[bR[b]])
                            r = ru % 2
                            ru += 1
                            cs_, sn_ = chain_tile(cos[:], dil, j), chain_tile(sin[:], dil, j)
                            view = (lambda a_: a_.rearrange("p (r m) -> p r m", r=4)) if dil == 16 else None
                            self.rope_apply(bank[b][:], bR[b], dst[:, j * 512:(j + 1) * 512], dR, cs_, sn_, tabR,
                                            qr[r][:], qrR[r], qs[r][:], qsR[r], t1[r][:], t1R[r], view=view)
                    self.ws.release()
                    wv, wR = self.ws.get(("xv", c, g))
                    wv = wv.rearrange("p (k c) -> p k c", k=KC)
                    for bi4 in range(4):
                        b = pu % 2
                        pu += 1
                        for bb in range(4):
                            bi = bi4 * 4 + bb
                            P.mm_group(bank[b][:, bb * 128:(bb + 1) * 128], [(chain_block(hn[:, k, :], dil, bi), wv[:, k, :]) for k in range(KC)],
                                       reads=[wR, hnR], writes=[bR[b]])
                        pv = bank[b][:].rearrange("p (b c) -> p b c", b=4)
                        P.op(P.act, lambda e: e.activation(out=vpad[:, bi4 * 4:(bi4 + 1) * 4, 0, 0:64], in_=pv[:, :, 0:64], func=AF.Identity),
                             reads=[bR[b]], writes=[vR])
                        P.op(P.act, lambda e: e.activation(out=vpad[:, bi4 * 4:(bi4 + 1) * 4, 1, 64:128], in_=pv[:, :, 64:128], func=AF.Identity),
                             reads=[bR[b]], writes=[vR])
                    self.ws.release()
                    def unit_info(qb):
                        r, b = qb // nb, qb % nb
                        kbs = [kb for kb in (b - 1, b, b + 1) if 0 <= kb < nb]
                        m0 = 128 * (kbs[0] - (b - 1))
                        return r, kbs, m0

                    def stage_a(qb):
                        nonlocal pi_
                        r, kbs, m0 = unit_info(qb)
                        nk = len(kbs)
                        pts = []
                        for hh in range(2):
                            hsl = slice(hh * 64, (hh + 1) * 64)
                            sb_ = 2 + hh
                            for jj, kb in enumerate(kbs):
                                kbi = r * nb + kb
                                P.op(P.pe, lambda e: e.matmul(bank[sb_][:, jj * 128:(jj + 1) * 128], kg[hsl, kbi * 128:(kbi + 1) * 128],
                                                              qg[hsl, qb * 128:(qb + 1) * 128], start=True, stop=True),
                                     reads=[kR, qR], writes=[bR[sb_]], signal=(jj == nk - 1))
                            pb = pi_ % 4
                            pi_ += 1
                            pts.append(pb)
                            P.op(P.act, lambda e: e.activation(out=pT[pb][:, 0:nk * 128], in_=bank[sb_][:, 0:nk * 128], func=AF.Exp, scale=0.125),
                                 reads=[bR[sb_]], writes=[pTR[pb]])
                            P.op(P.dve, lambda e: e.tensor_tensor(out=pT[pb][:, 0:nk * 128], in0=pT[pb][:, 0:nk * 128],
                                                                  in1=maskt[:, m0:m0 + nk * 128], op=ALU.mult),
                                 reads=[pTR[pb], maskR], writes=[pTR[pb]])
                        return pts

                    def stage_b(qb, pts, ab):
                        r, kbs, m0 = unit_info(qb)
                        nk = len(kbs)
                        qq = qb % 4
                        for (acc, accR, lh) in ((bank[4 + ab], bR[4 + ab], None), (bank[6 + ab], bR[6 + ab], onesab)):
                            tot = 2 * nk
                            ii = 0
                            for hh in range(2):
                                pb = pts[hh]
                                for jj, kb in enumerate(kbs):
                                    kbi = r * nb + kb
                                    lhsT = vpad[:, kbi, hh, :] if lh is None else onesab[:, hh, :]
                                    P.op(P.pe, lambda e: e.matmul(acc[:, qq * 128:(qq + 1) * 128], lhsT, pT[pb][:, jj * 128:(jj + 1) * 128],
                                                                  start=(ii == 0), stop=(ii == tot - 1)),
                                         reads=[pTR[pb], vR, onesR], writes=[accR], signal=(ii == tot - 1))
                                    ii += 1

                    nxt = stage_a(0)
                    for qb in range(16):
                        j = qb // 4
                        if qb % 4 == 0:
                            ab = au % 2
                            au += 1
                        cur = nxt
                        if qb + 1 < 16:
                            nxt = stage_a(qb + 1)
                        stage_b(qb, cur, ab)
                        if qb % 4 != 3:
                            continue
                        nps, dps = bank[4 + ab], bank[6 + ab]
                        for (acc, accR, dstt, dstR) in ((nps, bR[4 + ab], nacc, naccR), (dps, bR[6 + ab], dacc, daccR)):
                            if g == 0:
                                P.op(P.act, lambda e: e.activation(out=dstt[:, j * 512:(j + 1) * 512], in_=acc[:], func=AF.Identity),
                                     reads=[accR], writes=[dstR])
                            else:
                                dv = chain_tile(dstt[:], dil, j)
                                av = acc[:].rearrange("p (r m) -> p r m", r=4) if dil == 16 else acc[:]
                                P.op(P.dve, lambda e: e.tensor_tensor(out=dv, in0=dv, in1=av, op=ALU.add),
                                     reads=[accR, dstR], writes=[dstR])
                P.op(P.dve, lambda e: e.reciprocal(out=dacc[:], in_=dacc[:]), reads=[daccR], writes=[daccR])
                P.op(P.dve, lambda e: e.tensor_tensor(out=oTp[:], in0=nacc[:], in1=dacc[:], op=ALU.mult),
                     reads=[naccR, daccR], writes=[oTR])
                wv, wR = self.ws.get(("xo", c))
                for m in range(KC):
                    for t in range(4):
                        ts = slice(t * 512, (t + 1) * 512)
                        b = pu % 2
                        pu += 1
                        P.op(P.pe, lambda e: e.matmul(bank[b][:], wv[:, m * 128:(m + 1) * 128], oTp[:, ts], start=True, stop=True),
                             reads=[wR, oTR], writes=[bR[b]])
                        P.op(P.dve, lambda e: e.tensor_tensor(out=self.h[:, m, ts], in0=self.h[:, m, ts], in1=bank[b][:], op=ALU.add),
                             reads=[bR[b], self.hR[m][t]], writes=[self.hR[m][t]])
                self.ws.release()
            P.barrier()

    def final_store(self, sq_idx, do_norm):
        P, nc = self.P, self.nc
        with ExitStack() as es:
            if do_norm:
                rstd = es.enter_context(nc.sbuf_tensor(f"{self.uq()}fn_rstd", [128, S], F32))
                sq = [es.enter_context(nc.sbuf_tensor(f"{self.uq()}fn_sq{i}", [128, 512], BF16)) for i in range(2)]
                ps_stat = es.enter_context(nc.psum_tensor(f"{self.uq()}fn_psstat", [128, 512], F32))
                rstdR = [Res(f"rstd{t}") for t in range(4)]
                sqR = [Res("sq0"), Res("sq1")]
                psR = Res("ps_stat")
                self.rms_stats(rstd, rstdR, ps_stat[:], psR, [s_[:] for s_ in sq], sqR)
            outd = self.dram["out"][sq_idx].rearrange("(k p) t -> p k t", p=128)
            for k in range(KC):
                if do_norm:
                    P.op(P.dve, lambda e, k=k: e.scalar_tensor_tensor(
                        out=self.h[:, k, :], in0=self.h[:, k, :], scalar=self.par("final_norm", k),
                        in1=rstd[:], op0=ALU.mult, op1=ALU.mult),
                        reads=[self.hR[k][t] for t in range(4)] + rstdR, writes=[self.hR[k][t] for t in range(4)])
                tok = P.dma(P.sp, outd[:, k, :], self.h[:, k, :], reads=[self.hR[k][t] for t in range(4)])
                self.out_toks.append(tok)
            P.barrier()

    def build(self, weight_shapes):
        nc = self.nc
        self.dram = {}
        for name, (shape, dt) in weight_shapes.items():
            self.dram[name] = nc.dram_tensor(name, list(shape), dt, kind="ExternalInput").ap()
        self.dram["out"] = nc.dram_tensor("out", [NSEQ, D, S], F32, kind="ExternalOutput").ap()
        with ExitStack() as es:
            self.P = P = Prog(nc, es)
            self.params = es.enter_context(nc.sbuf_tensor(f"{self.uq()}sb_params", [128, self.npar], F32))
            self.ones_bf = es.enter_context(nc.sbuf_tensor(f"{self.uq()}ones_bf", [128, 128], BF16))
            self.h = es.enter_context(nc.sbuf_tensor(f"{self.uq()}h", [128, KC, S], F32))
            ring_t = es.enter_context(nc.sbuf_tensor(f"{self.uq()}wring", [128, NSLOT, SLOT], BF16))
            self.hR = [[Res(f"h{k}_{t}") for t in range(4)] for k in range(KC)]
            self.ws = WStream(P, ring_t, NSLOT)
            self.out_toks = []
            self.plan_weights()
            parR = Res("params")
            onesR = Res("ones")
            P.dma(P.sp, self.params[:], self.dram["params"], writes=[parR])
            P.op(P.dve, lambda e: e.memset(self.ones_bf[:], 1.0), writes=[onesR])
            self.ident_bf = es.enter_context(nc.sbuf_tensor(f"{self.uq()}ident_bf", [128, 128], BF16))
            P.dma(P.pool, self.ident_bf[:], self.dram["ident"], writes=[Res("ident")])
            P.barrier()
            self.ws.release()
            for sq_idx in range(self.nseq):
                xin = self.dram["xT"][sq_idx].rearrange("(k p) t -> p k t", p=128)
                for k in range(KC):
                    P.dma(P.sp, self.h[:, k, :], xin[:, k, :], writes=[self.hR[k][t] for t in range(4)])
                stopped = False
                for li in range(4):
                    if li in self.mixers:
                        if li == 0:
                            self.pool_layer(li)
                        elif li == 1:
                            self.diff_layer(li)
                        elif li == 2:
                            self.lru_layer(li)
                        else:
                            self.dil_layer(li)
                    if li in self.ffns:
                        self.ffn_layer(li)
                    if self.stop_after is not None and li >= self.stop_after:
                        stopped = True
                        break
                self.final_store(sq_idx, do_norm=not stopped)
            for tok in self.out_toks:
                P._wait(P.sp, *tok)
            P.barrier()
        return nc


def host_consts():
    t = np.arange(S)
    inv = np.zeros((4, S), np.float32)
    for g, win in enumerate((2, 4, 8, 16)):
        lo = np.clip(t - win // 2, 0, S)
        hi = np.clip(t + win - win // 2, 0, S)
        inv[g] = 1.0 / (hi - lo).astype(np.float32)
    edge = np.concatenate([inv[:, 0:8], inv[:, S - 8:S]], axis=1)
    i = np.arange(128)[:, None]
    j = np.arange(128)[None, :]
    mask = np.concatenate([(i - j >= 64), (np.abs(i - j) <= 64), (i - j <= -64)], axis=1).astype(np.float32)
    return {"invcnt": np.ascontiguousarray(np.broadcast_to(edge[None], (128, 4, 16))),
            "ident": np.eye(128, dtype=np.float32),
            "dil_mask": np.ascontiguousarray(mask)}


_CACHE = {}


def kernel(stop_after=None, _mixers=(0, 1, 2, 3), _ffns=(0, 1, 2, 3), **inp):
    inp = {k: np.asarray(v) for k, v in inp.items()}
    pp = pack_params(inp)
    params = pp.build()
    consts = host_consts()
    x = inp["x"]
    xT = np.ascontiguousarray(np.transpose(x, (0, 2, 1)))
    shared = {
        "params": params,
        "invcnt": consts["invcnt"],
        "pool_w": np.ascontiguousarray(inp["pool_w"], dtype=np.float32),
        "ffn_w_up": np.ascontiguousarray(inp["ffn_w_up"], dtype=np.float32),
        "ffn_w_down": np.ascontiguousarray(inp["ffn_w_down"], dtype=np.float32),
        "ident": consts["ident"],
        "dil_mask": consts["dil_mask"],
        "pos_b": np.ascontiguousarray(np.broadcast_to(inp["positions"].astype(np.int32)[None, :], (128, S))),
        "diff_w_qkv_p": permute_qk_cols(inp["diff_w_qkv"][0], [b * 64 for b in range(32)]),
        "diff_w_o": np.ascontiguousarray(inp["diff_w_o"], dtype=np.float32),
        "lru_w_in": np.ascontiguousarray(inp["lru_w_in"], dtype=np.float32),
        "lru_w_out": np.ascontiguousarray(inp["lru_w_out"], dtype=np.float32),
        "lru_bd": lru_blockdiag(inp["lru_w_a"][0], inp["lru_w_x"][0]),
        "dil_w_qkv_p": permute_qk_cols(inp["dil_w_qkv"][0], [((g * 3 + t) * 16 + hd) * 64 for g in range(3) for t in range(2) for hd in range(16)]),
        "dil_w_o": np.ascontiguousarray(inp["dil_w_o"], dtype=np.float32),
    }
    shapes = {k: (v.shape, I32 if v.dtype == np.int32 else F32) for k, v in shared.items()}
    shapes["xT"] = ((NSEQ, D, S), F32)
    key = (stop_after, params.shape[1])
    b = Builder(pp.cols, params.shape[1], stop_after=stop_after, mixers=_mixers, ffns=_ffns)
    import os as _os
    if _os.environ.get("K_DBG_STAGE"):
        b.dbg_stage = int(_os.environ["K_DBG_STAGE"])
    nc = b.build(shapes)
    in_maps = []
    for c in range(NCORES):
        m = dict(shared)
        m["xT"] = xT[c * NSEQ:(c + 1) * NSEQ]
        in_maps.append(m)
    res = run_bass_kernel_spmd(nc, in_maps, core_ids=list(range(NCORES)))
    outT = np.concatenate([r["out"] for r in res.results], axis=0)
    return np.ascontiguousarray(np.transpose(outT, (0, 2, 1))).astype(np.float32)
```
